# Optimizing a Trainium2 kernel written in Bass

```python
import jax, jax.numpy as jnp
from jax import lax
import numpy as np

D_MODEL = 1024
BATCH = 16
SEQ = 2048
DEPTH = 1
DEC_BATCH = 4
DEC_SEQ = 4096
PAST_LEN = 128

FNET_WIDTH = 512
FNET_GROUPS = 4
FNET_GROUP_DIM = FNET_WIDTH // FNET_GROUPS
SGU_WIDTH = 512
SGU_HEADS = 4
SGU_HEAD_DIM = SGU_WIDTH // SGU_HEADS
CHUNK = 128
EPS = 1e-6

IN_SIZES = (FNET_WIDTH, FNET_WIDTH, SGU_WIDTH, SGU_WIDTH, SGU_WIDTH, D_MODEL, D_MODEL)
IN_WIDTH = sum(IN_SIZES)
IN_SPLITS = tuple(int(v) for v in np.cumsum(IN_SIZES)[:-1])

kernel_name = "fnet_gmlp_gated_hybrid_encoder"


def rmsnorm(x, g):
    xf = x.astype(jnp.float32)
    y = xf * lax.rsqrt(jnp.mean(xf * xf, axis=-1, keepdims=True) + EPS)
    return (y * g.astype(jnp.float32)).astype(x.dtype)


def layernorm(x, g, b):
    xf = x.astype(jnp.float32)
    mu = jnp.mean(xf, axis=-1, keepdims=True)
    xc = xf - mu
    var = jnp.mean(xc * xc, axis=-1, keepdims=True)
    y = xc * lax.rsqrt(var + EPS) * g.astype(jnp.float32) + b.astype(jnp.float32)
    return y.astype(x.dtype)


def fourier_branch(a, w_fmix, b_fmix):
    B, S, _ = a.shape
    a4 = a.reshape(B, S, FNET_GROUPS, FNET_GROUP_DIM).astype(jnp.float32)
    f = jnp.fft.fft2(a4, axes=(1, 3), norm="ortho").real.astype(a.dtype)
    y = jnp.einsum("bsgc,gcd->bsgd", f, w_fmix) + b_fmix
    return y.reshape(B, S, FNET_WIDTH)


def sgu_branch(u, v, ln_g, ln_b, w_s, b_s):
    B, S, _ = v.shape
    v = layernorm(v, ln_g, ln_b)
    vc = v.reshape(B, S // CHUNK, CHUNK, SGU_HEADS, SGU_HEAD_DIM)
    mixed = jnp.einsum("hpq,bnqhc->bnphc", w_s, vc) + b_s.T[:, :, None]
    return u * mixed.reshape(B, S, SGU_WIDTH)


def layer(x, c, norm_g, w_ada, b_ada, w_in, w_fmix, b_fmix, sgu_ln_g, sgu_ln_b,
          w_s, b_s, w_pa, w_pb, w_out):
    mod = jax.nn.silu(c) @ w_ada + b_ada
    shift, scale, gate = jnp.split(mod, 3, axis=-1)
    h = rmsnorm(x, norm_g) * (1.0 + scale[:, None, :]) + shift[:, None, :]
    z = h @ w_in
    a, ga, u, v, gb, ma, mb = jnp.split(z, IN_SPLITS, axis=-1)
    y_a = fourier_branch(a, w_fmix, b_fmix) * jax.nn.silu(ga)
    y_b = sgu_branch(u, v, sgu_ln_g, sgu_ln_b, w_s, b_s) * jax.nn.silu(gb)
    merged = jax.nn.sigmoid(ma) * (y_a @ w_pa) + jax.nn.sigmoid(mb) * (y_b @ w_pb)
    out = merged @ w_out
    return x + gate[:, None, :] * out


def trunk(x, c, norm_g, w_ada, b_ada, w_in, w_fmix, b_fmix, sgu_ln_g, sgu_ln_b,
          w_s, b_s, w_pa, w_pb, w_out, final_g):
    for l in range(DEPTH):
        x = layer(x, c, norm_g[l], w_ada[l], b_ada[l], w_in[l], w_fmix[l], b_fmix[l],
                  sgu_ln_g[l], sgu_ln_b[l], w_s[l], b_s[l], w_pa[l], w_pb[l], w_out[l])
    return rmsnorm(x, final_g)


def setup_inputs(seed: int = 0) -> dict:
    key = jax.random.key(seed)
    ks = jax.random.split(key, 20)
    f32 = jnp.float32
    nrm = lambda k, shape, s: jax.random.normal(k, shape, f32) * s
    return {
        "x_prompt": nrm(ks[0], (BATCH, SEQ, D_MODEL), 1.0),
        "x_sample": nrm(ks[1], (DEC_BATCH, DEC_SEQ, D_MODEL), 1.0),
        "c_prompt": nrm(ks[2], (BATCH, D_MODEL), 1.0),
        "c_sample": nrm(ks[3], (DEC_BATCH, D_MODEL), 1.0),
        "norm_g": 1.0 + nrm(ks[4], (DEPTH, D_MODEL), 0.02),
        "w_ada": nrm(ks[5], (DEPTH, D_MODEL, 3 * D_MODEL), 0.3 * D_MODEL ** -0.5),
        "b_ada": nrm(ks[6], (DEPTH, 3 * D_MODEL), 0.02),
        "w_in": nrm(ks[7], (DEPTH, D_MODEL, IN_WIDTH), D_MODEL ** -0.5),
        "w_fmix": nrm(ks[8], (DEPTH, FNET_GROUPS, FNET_GROUP_DIM, FNET_GROUP_DIM), FNET_GROUP_DIM ** -0.5),
        "b_fmix": nrm(ks[9], (DEPTH, FNET_GROUPS, FNET_GROUP_DIM), 0.02),
        "sgu_ln_g": 1.0 + nrm(ks[10], (DEPTH, SGU_WIDTH), 0.02),
        "sgu_ln_b": nrm(ks[11], (DEPTH, SGU_WIDTH), 0.02),
        "w_s": nrm(ks[12], (DEPTH, SGU_HEADS, CHUNK, CHUNK), CHUNK ** -0.5),
        "b_s": 1.0 + nrm(ks[13], (DEPTH, SGU_HEADS, CHUNK), 0.02),
        "w_pa": nrm(ks[14], (DEPTH, FNET_WIDTH, D_MODEL), FNET_WIDTH ** -0.5),
        "w_pb": nrm(ks[15], (DEPTH, SGU_WIDTH, D_MODEL), SGU_WIDTH ** -0.5),
        "w_out": nrm(ks[16], (DEPTH, D_MODEL, D_MODEL), D_MODEL ** -0.5),
        "final_g": 1.0 + nrm(ks[17], (D_MODEL,), 0.02),
    }


def reference(x_prompt, x_sample, c_prompt, c_sample, norm_g, w_ada, b_ada, w_in, w_fmix, b_fmix,
              sgu_ln_g, sgu_ln_b, w_s, b_s, w_pa, w_pb, w_out, final_g):
    y_prompt = trunk(x_prompt, c_prompt, norm_g, w_ada, b_ada, w_in, w_fmix, b_fmix, sgu_ln_g,
                     sgu_ln_b, w_s, b_s, w_pa, w_pb, w_out, final_g)
    y_sample = trunk(x_sample, c_sample, norm_g, w_ada, b_ada, w_in, w_fmix, b_fmix, sgu_ln_g,
                     sgu_ln_b, w_s, b_s, w_pa, w_pb, w_out, final_g)
    return (y_prompt, y_sample)
```

```python
import numpy as np
import ml_dtypes
import concourse.bass as bass
import concourse.mybir as mybir
from concourse.bass_utils import run_bass_kernel_spmd

F32 = mybir.dt.float32
BF16 = mybir.dt.bfloat16
AF = mybir.ActivationFunctionType
ALU = mybir.AluOpType

D = 1024
INW = 4608
EPS = 1e-6
NCORES = 8
UT = 2048
A_OFF, GA_OFF, U_OFF, V_OFF, GB_OFF, MA_OFF, MB_OFF = 0, 512, 1024, 1536, 2048, 2560, 3584


class Sched:
    ENGS = ("pe", "act", "dve", "pool", "sp")

    def __init__(self):
        self.streams = {e: [] for e in self.ENGS}
        self.count = {}
        self.waited = {e: {} for e in self.ENGS}
        self.last_w = {}
        self.readers = {}
        self.phase_users = {}

    def _waits(self, eng, reads, writes, phase, extra):
        deps = {}

        def add(tok, raw):
            sk, v = tok
            if sk == eng and (eng == "pe" or not raw):
                return
            if deps.get(sk, 0) < v:
                deps[sk] = v

        for b in reads:
            t = self.last_w.get(b)
            if t:
                add(t, True)
        for b in writes:
            t = self.last_w.get(b)
            if t:
                add(t, False)
            for r in self.readers.get(b, ()):
                add(r, False)
        for t in extra:
            add(t, True)
        for (res, ph) in phase:
            prev = self.phase_users.get((res, ph - 1))
            if prev:
                for sk, v in prev.items():
                    add((sk, v), False)
        for sk, v in deps.items():
            if self.waited[eng].get(sk, 0) < v:
                self.streams[eng].append(("wait", sk, v))
                self.waited[eng][sk] = v

    def _update(self, reads, writes, phase, tok):
        for b in reads:
            self.readers.setdefault(b, []).append(tok)
        for b in writes:
            self.last_w[b] = tok
            self.readers[b] = []
        for key in phase:
            d = self.phase_users.setdefault(key, {})
            if d.get(tok[0], 0) < tok[1]:
                d[tok[0]] = tok[1]

    def op(self, eng, fn, reads=(), writes=(), phase=(), extra=()):
        self._waits(eng, reads, writes, phase, extra)
        self.count[eng] = self.count.get(eng, 0) + 1
        tok = (eng, self.count[eng])
        self.streams[eng].append(("op", fn, eng, 1))
        self._update(reads, writes, phase, tok)
        return tok

    def pe(self, fns, reads=(), writes=(), phase=(), extra=()):
        self._waits("pe", reads, writes, phase, extra)
        for f in fns[:-1]:
            self.streams["pe"].append(("raw", f))
        self.count["pe"] = self.count.get("pe", 0) + 1
        tok = ("pe", self.count["pe"])
        self.streams["pe"].append(("op", fns[-1], "pe", 1))
        self._update(reads, writes, phase, tok)
        return tok

    def dma(self, queue, fns, sem, reads=(), writes=(), phase=(), extra=()):
        self._waits(queue, reads, writes, phase, extra)
        for f in fns:
            self.count[sem] = self.count.get(sem, 0) + 16
            self.streams[queue].append(("op", f, sem, 16))
        tok = (sem, self.count[sem])
        self._update(reads, writes, phase, tok)
        return tok


def _consts(hf):
    c = {}
    c["identb"] = np.eye(128, dtype=np.float32).astype(ml_dtypes.bfloat16)
    c["identf"] = np.eye(128, dtype=np.float32)
    c["onesf"] = np.ones((128, 128), dtype=np.float32)
    cc = np.arange(128)
    ph = 2.0 * np.pi * np.outer(cc, cc) / 128.0
    c["cosC"] = np.cos(ph).astype(np.float32)
    c["sinC"] = np.sin(ph).astype(np.float32)
    r2 = np.zeros((128, 256), dtype=np.float64)
    for s1j in range(8):
        for s2 in range(16):
            p = 16 * s1j + s2
            for hh in range(2):
                for k2l in range(8):
                    k2 = 8 * hh + k2l
                    ang = 2.0 * np.pi * s2 * k2 / 16.0
                    r2[p, hh * 128 + 0 * 64 + k2l * 8 + s1j] = np.cos(ang)
                    r2[p, hh * 128 + 1 * 64 + k2l * 8 + s1j] = -np.sin(ang)
    c["r2h"] = r2.astype(np.float32)
    for u, (S, off) in enumerate(((4096, 128 * hf), (2048, 0))):
        N1 = S // 16
        KC = N1 // 128
        scale = 1.0 / np.sqrt(S * 128.0)
        s1 = (128 * np.arange(KC)[None, :] + np.arange(128)[:, None]).astype(np.float64)
        k = 16.0 * (np.arange(128) + off)[None, :] + np.arange(16)[:, None]
        th = 2.0 * np.pi * s1[None, :, :, None] * k[:, None, None, :] / S
        E = np.stack([np.cos(th), np.sin(th)], axis=3) * scale
        c["Ek%d" % u] = E.reshape(16 * 128, KC * 2 * 128).astype(np.float32)
    return c


_CONST_SHAPES = None


def build_nc(stage=0):
    nc = bass.Bass("TRN2", target_bir_lowering=False)
    S = Sched()

    def din(name, shape, dt=F32):
        return nc.dram_tensor(name, list(shape), dt, kind="ExternalInput")

    xs_full = din("xs_full", [4096, D])
    xu = din("xu", [3 * UT, D])
    cT = din("cT", [128, 24])
    w_ada = din("w_ada", [D, 3 * D])
    b_adaT = din("b_adaT", [128, 24])
    norm_gT = din("norm_gT", [128, 8])
    w_in = din("w_in", [D, INW])
    w_pa = din("w_pa", [512, D])
    w_pb = din("w_pb", [512, D])
    w_out = din("w_out", [D, D])
    wfm = din("wfm", [128, 512])
    b_fmixT = din("b_fmixT", [128, 4])
    ln_gT = din("ln_gT", [128, 4])
    ln_b = din("ln_b", [512])
    w_s = din("w_s", [512, 128])
    b_s = din("b_s", [512])
    final_g = din("final_g", [D])
    identb = din("identb", [128, 128], BF16)
    identf = din("identf", [128, 128])
    onesf = din("onesf", [128, 128])
    cosC = din("cosC", [128, 128])
    sinC = din("sinC", [128, 128])
    r2h = din("r2h", [128, 256])
    Ek0 = din("Ek0", [16 * 128, 2 * 2 * 128])
    Ek1 = din("Ek1", [16 * 128, 1 * 2 * 128])
    yu = nc.dram_tensor("yu", [3 * UT, D], F32, kind="ExternalOutput")

    from contextlib import ExitStack
    es = ExitStack()

    def sb(name, shape, dt):
        return es.enter_context(nc.sbuf_tensor(name, list(shape), dt))

    w_in_sb = sb("w_in_sb", [128, 8, INW], BF16)
    w_pa_sb = sb("w_pa_sb", [128, 4, D], BF16)
    w_pb_sb = sb("w_pb_sb", [128, 4, D], BF16)
    w_out_sb = sb("w_out_sb", [128, 8, D], BF16)
    AY = sb("AY", [128, 16384], BF16)
    Y0 = sb("Y0", [128, 2048], BF16)
    AR = sb("AR", [128, 8192], BF16)
    hT0 = sb("hT0", [128, 8, 512], BF16)
    xsr = sb("xsr", [128, 4, D], BF16)
    NXY = 4
    xy = sb("xy", [128, NXY, D], F32)
    gate_bc = sb("gate_bc", [128, D], F32)
    fg_bc = sb("fg_bc", [128, D], F32)
    csw = sb("csw", [128, 4, 384], BF16)
    identb_sb = sb("identb_sb", [128, 128], BF16)
    identf_sb = sb("identf_sb", [128, 128], F32)
    onesf_sb = sb("onesf_sb", [128, 128], F32)
    dgt = sb("dgt", [128, 2, 128], F32)
    r2_sb = sb("r2_sb", [128, 256], BF16)
    wsT = sb("wsT", [128, 4, 128], BF16)
    Ktab = sb("Ktab", [128, 4, 128], BF16)
    cT_sb = sb("cT_sb", [128, 24], F32)
    cs_sb = sb("cs_sb", [128, 24], F32)
    badaT_sb = sb("badaT_sb", [128, 24], F32)
    ngT_sb = sb("ngT_sb", [128, 8], F32)
    modT = sb("modT", [128, 24, 3], F32)
    Gtab = sb("Gtab", [128, 8, 3], F32)
    Gtmp = sb("Gtmp", [128, 8, 3], F32)
    bfm_sb = sb("bfm_sb", [128, 4], F32)
    lng_sb = sb("lng_sb", [128, 4], F32)
    ARf = AR[:, 0:3584].bitcast(F32)
    cosC_sb = ARf[:, 0:128]
    sinC_sb = ARf[:, 128:256]
    wfm_sb = ARf[:, 256:768]
    lnb_f = ARf[:, 768:1280]
    bs_bc = ARf[:, 1280:1792]
    lnb_b = AR[:, 3584:4096]
    ws_nat = AR[:, 4096:4608].rearrange("p (h q) -> p h q", h=4)
    junk = AY[:, 6144:7168]
    eps_sb = sb("eps_sb", [128, 1], F32)
    stat = sb("stat", [128, 8, 4], F32)
    bnst = sb("bnst", [128, 4, 6], F32)
    bnag = sb("bnag", [128, 4, 4], F32)
    ps = [es.enter_context(nc.psum_tensor("ps%d" % i, [128, 512], F32)) for i in range(8)]

    semnames = ["pe", "act", "dve", "pool", "s_win", "s_wina", "s_wrest", "s_wpab", "s_tab", "s_wada0", "s_wada1",
                "s_ld0", "s_ld1", "s_ld2", "s_ld3", "s_ld4", "s_ld5", "s_ld6", "s_ek0", "s_ek1", "s_ek2", "s_ek3", "s_ek4", "s_ek5", "s_ek6", "s_ek7", "s_st0", "s_st1", "s_st2", "s_st3", "s_wout"]
    sems = {n: es.enter_context(nc.semaphore(n)) for n in semnames}

    bank_ctr = [0]
    bank_ring = list(range(8))

    def newbank():
        b = bank_ring[bank_ctr[0] % len(bank_ring)]
        bank_ctr[0] += 1
        return b

    def psb(b):
        return ps[b][:].bitcast(BF16)

    def dram_ap(t, offset, dims):
        return bass.AP(t.ap().tensor, offset, [list(d) for d in dims])

    tabs = []

    def tab(dst_ap, src_ap):
        tabs.append(lambda e, d=dst_ap, s=src_ap: e.dma_start(out=d, in_=s))

    tab(cT_sb[:], cT.ap())
    tab(badaT_sb[:], b_adaT.ap())
    tab(ngT_sb[:], norm_gT.ap())
    tab(identf_sb[:], identf.ap())
    tab(onesf_sb[:], onesf.ap())
    tab(identb_sb[:], identb.ap())
    tab(cosC_sb, cosC.ap())
    tab(sinC_sb, sinC.ap())
    tab(wfm_sb, wfm.ap())
    tab(bfm_sb[:], b_fmixT.ap())
    tab(lng_sb[:], ln_gT.ap())
    tab(lnb_f, dram_ap(ln_b, 0, [[0, 128], [1, 512]]))
    tab(bs_bc, dram_ap(b_s, 0, [[0, 128], [1, 512]]))
    tab(fg_bc[:], dram_ap(final_g, 0, [[0, 128], [1, D]]))
    TABKEYS = ["tabs"]
    S.dma("sp", tabs, "s_tab", writes=TABKEYS, phase=[("AR", 0)])

    AYf = AY[:].bitcast(F32)
    wada_blk = [AYf[:, 0:4096].rearrange("p (k n) -> p k n", k=8),
                AYf[:, 4096:8192].rearrange("p (k n) -> p k n", k=8)]

    def wada_dma(cb):
        src = dram_ap(w_ada, cb * 512, [[3 * D, 128], [128 * 3 * D, 8], [1, 512]])
        return lambda e, s=src, d=wada_blk[cb % 2]: e.dma_start(out=d, in_=s)

    wla, wl = [], []
    for kc in range(8):
        wla.append(lambda e, kc=kc: e.dma_start(out=w_in_sb[:, kc, 0:512], in_=w_in.ap()[kc * 128:(kc + 1) * 128, 0:512]))
    for cb in range(2):
        for kc in range(8):
            wl.append(lambda e, kc=kc, cb=cb: e.dma_start(
                out=w_in_sb[:, kc, 512 + cb * 2048: 512 + (cb + 1) * 2048],
                in_=w_in.ap()[kc * 128:(kc + 1) * 128, 512 + cb * 2048: 512 + (cb + 1) * 2048]))
    wr = []
    wpab = []
    for k in range(4):
        wpab.append(lambda e, k=k: e.dma_start(out=w_pa_sb[:, k, :], in_=w_pa.ap()[k * 128:(k + 1) * 128, :]))
        wpab.append(lambda e, k=k: e.dma_start(out=w_pb_sb[:, k, :], in_=w_pb.ap()[k * 128:(k + 1) * 128, :]))
    wr.append(lambda e: e.dma_start(out=r2_sb[:], in_=r2h.ap()))
    wr.append(lambda e: e.dma_start(out=ws_nat, in_=dram_ap(w_s, 0, [[128, 128], [128 * 128, 4], [1, 128]])))

    S.op("dve", lambda e: e.memset(eps_sb[:], EPS), writes=["eps"])
    S.op("act", lambda e: e.activation(out=cs_sb[:], in_=cT_sb[:], func=AF.Silu),
         reads=TABKEYS, writes=["cs"])
    bmod = newbank()
    for cb in range(6):
        S.dma("act", [wada_dma(cb)], "s_wada%d" % (cb % 2), writes=[("wada", cb % 2)], phase=[("AYU", 0)])
        fns = []
        for ch in range(4):
            chunk = cb * 4 + ch
            for kc in range(8):
                fns.append(lambda e, chunk=chunk, ch=ch, kc=kc, cb=cb: e.matmul(
                    ps[bmod][:, chunk * 3:chunk * 3 + 3],
                    lhsT=wada_blk[cb % 2][:, kc, ch * 128:(ch + 1) * 128],
                    rhs=cs_sb[:, kc * 3:kc * 3 + 3],
                    start=(kc == 0), stop=(kc == 7)))
        S.pe(fns, reads=[("wada", cb % 2), "cs"], writes=[("ps", bmod)], phase=[("AYU", 0)])
    S.dma("pool", wla, "s_wina", writes=["w_in_a"])
    S.dma("pool", wr, "s_wrest", writes=["w_rest"], phase=[("AR", 0)])

    S.op("dve", lambda e: e.tensor_tensor(out=modT[:], in0=ps[bmod][:, 0:72].rearrange("p (c b) -> p c b", b=3),
                                          in1=badaT_sb[:].unsqueeze(2).to_broadcast([128, 24, 3]), op=ALU.add),
         reads=[("ps", bmod), "tabs"], writes=["modT"])
    S.op("dve", lambda e: e.tensor_scalar(out=Gtmp[:], in0=modT[:, 8:16, :], scalar1=1.0, scalar2=None, op0=ALU.add),
         reads=["modT"], writes=["Gtmp"])
    S.op("dve", lambda e: e.tensor_tensor(out=Gtab[:], in0=Gtmp[:],
                                          in1=ngT_sb[:].unsqueeze(2).to_broadcast([128, 8, 3]), op=ALU.mult),
         reads=["Gtmp", "tabs"], writes=["Gtab"])

    P0 = [("AR", 0)]
    for g in range(4):
        b = newbank()
        S.pe([lambda e, g=g, b=b: e.matmul(ps[b][:, 0:128], lhsT=cosC_sb, rhs=wfm_sb[:, g * 128:(g + 1) * 128],
                                           start=True, stop=True),
              lambda e, g=g, b=b: e.matmul(ps[b][:, 128:256], lhsT=sinC_sb, rhs=wfm_sb[:, g * 128:(g + 1) * 128],
                                           start=True, stop=True)],
             reads=["tabs"], writes=[("ps", b)], phase=P0)
        S.op("act", lambda e, g=g, b=b: e.activation(out=csw[:, g, 128:256], in_=ps[b][:, 0:128], func=AF.Copy),
             reads=[("ps", b)], writes=[("csw", g)])
        S.op("act", lambda e, g=g, b=b: e.activation(out=csw[:, g, 256:384], in_=ps[b][:, 128:256], func=AF.Copy, scale=-1.0),
             reads=[("ps", b)], writes=[("csw", g)])
        S.op("act", lambda e, g=g, b=b: e.activation(out=csw[:, g, 0:128], in_=ps[b][:, 128:256], func=AF.Copy),
             reads=[("ps", b)], writes=[("csw", g)])

    bt = newbank()
    S.pe([lambda e, h=h: e.transpose(psb(bt)[:, h * 128:(h + 1) * 128], ws_nat[:, h, :], identb_sb[:]) for h in range(4)],
         reads=["w_rest", "tabs"], writes=[("ps", bt)], phase=P0)
    S.op("act", lambda e: e.activation(out=wsT[:].rearrange("p h q -> p (h q)"), in_=psb(bt)[:, 0:512], func=AF.Copy),
         reads=[("ps", bt)], writes=["wsT"])
    S.op("dve", lambda e: e.tensor_copy(out=lnb_b, in_=lnb_f), reads=["tabs"], writes=["lnb_b"], phase=P0)
    bk = newbank()
    S.pe([lambda e, h=h: e.matmul(ps[bk][:, h * 128:(h + 1) * 128], lhsT=lnb_b[:, h * 128:(h + 1) * 128], rhs=wsT[:, h, :],
                                  start=True, stop=True) for h in range(4)],
         reads=["lnb_b", "wsT"], writes=[("ps", bk)], phase=P0)
    S.op("dve", lambda e: e.tensor_tensor(out=Ktab[:].rearrange("p h q -> p (h q)"), in0=ps[bk][:, 0:512], in1=bs_bc, op=ALU.add),
         reads=[("ps", bk), "tabs"], writes=["Ktab"], phase=P0)

    fe_ctr = [0]
    be_ctr = [0]
    st_ctr = [0]
    FE_RING_A = [0, 1, 4, 5, 6]
    FE_RING_B = [0, 1]
    BE_RING = [2, 3]

    def xys(k):
        if k < NXY:
            return xy[:, k, :]
        if k == 6:
            return Y0[:].bitcast(F32)
        return AR[:, 4096 + 2048 * (k - 4): 4096 + 2048 * (k - 3)].bitcast(F32)

    def fe1(row_aps, ring, phase=(), between=None):
        for j, src in enumerate(row_aps):
            if between is not None:
                between(j)
            k = ring[fe_ctr[0] % len(ring)]
            fe_ctr[0] += 1
            si = st_ctr[0] % 8
            st_ctr[0] += 1
            ph = phase if k in (4, 5) else ()
            xk = [("xy", k)] + ([("Y", 0)] if k == 6 else [])
            S.dma("sp", [lambda e, k=k, src=src: e.dma_start(out=xys(k), in_=src)], "s_ld%d" % k,
                  writes=xk, phase=ph)
            S.op("act", lambda e, k=k, j=j, si=si: e.activation(out=xsr[:, j, :], in_=xys(k), func=AF.Square,
                                                                 accum_out=stat[:, si, 0:1]),
                 reads=xk, writes=[("xs", j), ("st", si)], phase=ph)
            S.op("act", lambda e, si=si: e.activation(out=stat[:, si, 1:2], in_=stat[:, si, 0:1], func=AF.Sqrt,
                                                       scale=1.0 / D, bias=eps_sb[:, 0:1]),
                 reads=[("st", si), "eps"], writes=[("st1", si)])
            S.op("dve", lambda e, si=si: e.reciprocal(out=stat[:, si, 2:3], in_=stat[:, si, 1:2]),
                 reads=[("st1", si)], writes=[("st2", si)])
            S.op("dve", lambda e, k=k, j=j, si=si: e.tensor_scalar(out=xsr[:, j, :], in0=xys(k),
                                                                   scalar1=stat[:, si, 2:3], scalar2=None, op0=ALU.mult),
                 reads=xk + [("st2", si)], writes=[("xs", j)], phase=ph)

    def fe2(hT, hkey, slot, phase=()):
        for kp in range(4):
            b = newbank()
            fns = []
            for kk in range(2):
                kc = 2 * kp + kk
                for j in range(4):
                    fns.append(lambda e, b=b, kk=kk, kc=kc, j=j: e.transpose(
                        psb(b)[:, kk * 512 + j * 128: kk * 512 + (j + 1) * 128],
                        xsr[:, j, kc * 128:(kc + 1) * 128], identb_sb[:]))
            S.pe(fns, reads=[("xs", j) for j in range(4)] + ["tabs"], writes=[("ps", b)])
            for kk in range(2):
                kc = 2 * kp + kk
                S.op("dve", lambda e, b=b, kk=kk, kc=kc: e.tensor_scalar(
                    out=hT[:, kc, :], in0=psb(b)[:, kk * 512:(kk + 1) * 512],
                    scalar1=Gtab[:, kc, slot:slot + 1], scalar2=modT[:, kc, slot:slot + 1],
                    op0=ALU.mult, op1=ALU.add),
                    reads=[("ps", b), "Gtab", "modT"], writes=[(hkey, kc)], phase=phase)

    hT1 = AR[:, 0:4096].rearrange("p (k n) -> p k n", k=8)
    hTs = [hT0, hT1]

    out_tokens = []

    def rowsA_of(uu, s):
        Sin_ = 4096 if uu == 0 else 2048
        N1_ = Sin_ // 16
        xsrc_ = xs_full if uu == 0 else xu
        xbase_ = 0 if uu == 0 else uu * UT
        return [dram_ap(xsrc_, (xbase_ + 8 * (4 * s + j)) * D, [[D, 8], [N1_ * D, 16], [1, D]]) for j in range(4)]

    prefetched = {}
    pending_x = []

    def do_unit(u):
        if stage == 1:
            return
        Sin = 4096 if u == 0 else 2048
        N1 = Sin // 16
        KC = N1 // 128
        NT = Sin // 128
        NS = NT // 4
        Ek_d = Ek0 if u == 0 else Ek1
        xsrc = xs_full if u == 0 else xu
        xbase = 0 if u == 0 else u * UT
        PA, PF, PB = 3 * u + 1, 3 * u + 2, 3 * u + 3
        A_tm = AY[:, 0:4 * NT * 128].rearrange("p (g b c) -> p g b c", g=4, b=NT)

        def yaT(g):
            return Y0[:, :] if g == 0 else AY[:, (g - 1) * 2048: g * 2048]

        def gate_pre(kc):
            S.op("dve", lambda e, kc=kc: e.tensor_scalar(out=dgt[:, kc % 2, :], in0=identf_sb[:],
                                                        scalar1=modT[:, 16 + kc, u:u + 1], scalar2=None, op0=ALU.mult),
                 reads=["modT", "tabs"], writes=[("dgt", kc % 2)])

        def gate_post(kc):
            b = newbank()
            S.pe([lambda e, b=b, kc=kc: e.matmul(ps[b][:, 0:128], lhsT=onesf_sb[:], rhs=dgt[:, kc % 2, :], start=True, stop=True)],
                 reads=[("dgt", kc % 2), "tabs"], writes=[("ps", b)])
            S.op("act", lambda e, b=b, kc=kc: e.activation(out=gate_bc[:, kc * 128:(kc + 1) * 128], in_=ps[b][:, 0:128], func=AF.Copy),
                 reads=[("ps", b)], writes=["gate_bc"])

        def rowsA(s):
            aps = []
            for j in range(4):
                s1b = 4 * s + j
                aps.append(dram_ap(xsrc, (xbase + 8 * s1b) * D, [[D, 8], [N1 * D, 16], [1, D]]))
            return aps

        def passA_mm(s, js=range(4)):
            hT = hTs[s % 2]
            hkey = "hT%d" % (s % 2)
            for j in js:
                s1b = 4 * s + j
                b = newbank()
                S.pe([lambda e, b=b, kc=kc, j=j, hT=hT: e.matmul(ps[b][:], lhsT=hT[:, kc, j * 128:(j + 1) * 128],
                                                                  rhs=w_in_sb[:, kc, A_OFF:A_OFF + 512],
                                                                  start=(kc == 0), stop=(kc == 7)) for kc in range(8)],
                     reads=[(hkey, kc) for kc in range(8)] + ["w_in_a"], writes=[("ps", b)],
                     phase=[("AR", PA)] if s % 2 == 1 else ())
                S.op("act", lambda e, b=b, s1b=s1b: e.activation(out=A_tm[:, :, s1b, :],
                                                                 in_=ps[b][:].rearrange("p (g c) -> p g c", g=4), func=AF.Copy),
                     reads=[("ps", b)], writes=[("A", g) for g in range(4)] + [("Y", g) for g in range(1, 4)],
                     phase=[("AYU", PA)])

        def rowsB(t):
            return [xu.ap()[u * UT + t * 512 + j * 128: u * UT + t * 512 + (j + 1) * 128, :] for j in range(4)]

        def fe2A(s):
            fe2(hTs[s % 2], "hT%d" % (s % 2), u, phase=[("AR", PA)] if s % 2 == 1 else ())

        if stage == 21:
            return
        if not prefetched.get(u):
            fe1(rowsA(0), FE_RING_A, [("AR", PA)])
        if stage == 22:
            return
        if not prefetched.get(u):
            fe2A(0)
        if stage == 23:
            return
        if NS > 1:
            fe1(rowsA(1), FE_RING_A, [("AR", PA)])
        gsteps = 8 // NS
        for s in range(NS):
            for q in range(gsteps):
                gate_pre(s * gsteps + q)
            if s + 1 < NS:
                fe2A(s + 1)
            else:
                fe2(hT0, "hT0", u)
            btw = (lambda j, s=s: passA_mm(s, [j]))
            if s + 2 < NS:
                fe1(rowsA(s + 2), FE_RING_A, [("AR", PA)], between=btw)
            elif s + 2 == NS:
                fe1(rowsB(0), FE_RING_A, [("AR", PA)], between=btw)
            else:
                passA_mm(s)
            for q in range(gsteps):
                gate_post(s * gsteps + q)
            if pending_x and s < 4:
                if s + 1 < 4:
                    pending_x[s + 1][0]()
                pending_x[s][1](AY[:, 12288:13312])
                if s == 3:
                    pending_x[:] = []
        nslot = 4 if KC == 2 else 2
        xyb = xy[:, 0:nslot, :].rearrange("p s d -> p (s d)").bitcast(BF16)
        Ekr = xyb[:, 0:16 * KC * 256].rearrange("p (r k i d) -> p r k i d", r=16, k=KC, i=2)
        EKK = ["Ek"] + [("xy", s_) for s_ in range(nslot)]
        S.dma("pool", [lambda e, q=q: e.dma_start(
            out=Ekr[:, 4 * q:4 * q + 4, :, :, :].rearrange("p r k i d -> p r (k i d)"),
            in_=dram_ap(Ek_d, 4 * q * 128 * KC * 256, [[KC * 256, 128], [128 * KC * 256, 4], [1, KC * 256]]))
            for q in range(4)], "s_ek0", writes=EKK)

        S.dma("pool", [lambda e, k=k: e.dma_start(out=w_out_sb[:, k, :], in_=w_out.ap()[k * 128:(k + 1) * 128, :])
                       for k in range(8)], "s_wout", writes=["w_out"])
        for k in range(8):
            S.op("pool", lambda e, k=k: e.tensor_tensor(out=w_out_sb[:, k, :], in0=w_out_sb[:, k, :], in1=gate_bc[:], op=ALU.mult),
                 reads=["w_out", "gate_bc"], writes=["w_out"])
        if u == 0:
            S.dma("pool", wl, "s_win", writes=["w_in"])
            S.dma("pool", wpab, "s_wpab", writes=["w_pab"])

        if stage == 2:
            return
        if stage == 31:
            return

        combos_gh_pre = [(g, hh) for g in range(4) for hh in range(2)]
        NG = 2
        if KC == 1:
            Gts = [AR[:, i * 16 * N1:(i + 1) * 16 * N1].rearrange("p (q s) -> p q s", q=16) for i in range(NG)]
            gkeys = [[("G", 0)], [("G", 1)]]
        else:
            Gts = [AR[:, 0:16 * N1].rearrange("p (q s) -> p q s", q=16),
                   xsr[:].rearrange("p j d -> p (j d)").rearrange("p (q s) -> p q s", q=16)]
            gkeys = [[("G", 0)], [("xs", j) for j in range(4)]]
        NR = 8
        Hq = AR[:, 4096:4096 + NR * KC * 256].rearrange("p (r k i d) -> p r k i d", r=NR, k=KC, i=2)
        def ekkeys(r):
            return EKK

        ev_ctr = [0]
        h_ctr = [0]
        PFA = [("AR", PF)]
        combos_gh = [(g, hh) for g in range(4) for hh in range(2)]

        def s2stage(ci):
            g, hh = combos_gh[ci]
            Gt = Gts[ci % NG]
            gkey = gkeys[ci % NG]
            for sb4 in range(NT // 4):
                b = newbank()
                S.pe([lambda e, b=b, bb=bb, sb4=sb4: e.matmul(
                    ps[b][:, bb * 128:(bb + 1) * 128], lhsT=A_tm[:, g, 4 * sb4 + bb, :],
                    rhs=r2_sb[:, hh * 128:(hh + 1) * 128], start=True, stop=True) for bb in range(4)],
                    reads=[("A", g), "w_rest"], writes=[("ps", b)], phase=[("AYU", PF)])
                src_ = ps[b][:].rearrange("p (b q j) -> p b q j", b=4, q=16)
                dst_ = Gt[:, :, 32 * sb4: 32 * sb4 + 32].rearrange("p q (b j) -> p b q j", b=4)
                if ev_ctr[0] % 4 == 0:
                    S.op("act", lambda e, src_=src_, dst_=dst_: e.activation(out=dst_, in_=src_, func=AF.Copy),
                         reads=[("ps", b)], writes=gkey, phase=PFA)
                else:
                    S.op("dve", lambda e, src_=src_, dst_=dst_: e.tensor_copy(out=dst_, in_=src_),
                         reads=[("ps", b)], writes=gkey, phase=PFA)
                ev_ctr[0] += 1

        def chanfinal(ci):
            g, hh = combos_gh[ci]
            Gt = Gts[ci % NG]
            gkey = gkeys[ci % NG]

            def channel(k2l):
                b = newbank()
                fns = []
                for kc in range(KC):
                    fns.append(lambda e, b=b, kc=kc, k2l=k2l: e.matmul(
                        ps[b][:, kc * 256:(kc + 1) * 256], lhsT=Gt[:, k2l, kc * 128:(kc + 1) * 128],
                        rhs=csw[:, g, 128:384], start=True, stop=False))
                    fns.append(lambda e, b=b, kc=kc, k2l=k2l: e.matmul(
                        ps[b][:, kc * 256:(kc + 1) * 256], lhsT=Gt[:, 8 + k2l, kc * 128:(kc + 1) * 128],
                        rhs=csw[:, g, 0:256], start=False, stop=True))
                S.pe(fns, reads=gkey + [("csw", g)], writes=[("ps", b)], phase=PFA)
                r = h_ctr[0] % NR
                h_ctr[0] += 1
                k2 = 8 * hh + k2l
                dsto = Hq[:, r, :, :, :].rearrange("p k i d -> p (k i d)")
                srco = ps[b][:, 0:KC * 256]
                if h_ctr[0] % 2 == 0:
                    S.op("act", lambda e, dsto=dsto, srco=srco: e.activation(out=dsto, in_=srco, func=AF.Copy),
                         reads=[("ps", b)], writes=[("H", r)], phase=PFA)
                else:
                    S.op("dve", lambda e, dsto=dsto, srco=srco: e.tensor_copy(out=dsto, in_=srco),
                         reads=[("ps", b)], writes=[("H", r)], phase=PFA)
                return r

            def final(k2l, r, bF):
                fns = []
                n = 0
                for kc in range(KC):
                    for i in range(2):
                        fns.append(lambda e, kc=kc, i=i, r=r, k2l=k2l, bF=bF, n=n: e.matmul(
                            ps[bF][:, (k2l % 4) * 128:(k2l % 4 + 1) * 128], lhsT=Hq[:, r, kc, i, :],
                            rhs=Ekr[:, 8 * hh + k2l, kc, i, :], start=(n == 0), stop=(n == 2 * KC - 1)))
                        n += 1
                S.pe(fns, reads=[("H", r)] + ekkeys(r), writes=[("ps", bF)], phase=PFA)
                if k2l % 4 == 3:
                    k2base = 8 * hh + k2l - 3
                    dst_ = yaT(g).rearrange("p (n k) -> p k n", k=16)[:, k2base:k2base + 4, :]
                    src_ = ps[bF][:].rearrange("p (k n) -> p k n", k=4)
                    S.op("act", lambda e, dst_=dst_, src_=src_: e.activation(out=dst_, in_=src_, func=AF.Identity,
                                                                             bias=bfm_sb[:, g:g + 1]),
                         reads=[("ps", bF), "tabs"], writes=[("Y", g)], phase=[("AYU", PF)])

            return channel, final

        LAG = 3
        pend = []
        bFs = {}

        fin_ctr = [0]

        def do_final(item):
            fin, pk, pr, ci = item
            if pk % 4 == 0:
                bFs[ci] = 6 + fin_ctr[0] % 2
                fin_ctr[0] += 1
            fin(pk, pr, bFs[ci])

        bank_ring[:] = [0, 1, 2, 3, 4, 5]
        s2stage(0)
        s2stage(1)
        for ci in range(8):
            channel, final = chanfinal(ci)
            for k2l in range(8):
                r = channel(k2l)
                pend.append((final, k2l, r, ci))
                if len(pend) > LAG:
                    do_final(pend.pop(0))
            if ci + 2 < 8:
                s2stage(ci + 2)
        while pend:
            do_final(pend.pop(0))
        bank_ring[:] = list(range(8))

        if stage == 3:
            return
        sga = AR[:, 0:2048].rearrange("p (f n) -> p f n", f=4)
        sgb = AR[:, 2048:4096].rearrange("p (f n) -> p f n", f=4)
        tyb = AR[:, 4096:6144].rearrange("p (f n) -> p f n", f=4)
        vhat = AR[:, 6144:8192].rearrange("p (j c) -> p j c", j=4)
        sma = AY[:, 8192:12288].rearrange("p (k n) -> p k n", k=8)
        smb = AY[:, 12288:16384].rearrange("p (k n) -> p k n", k=8)
        phB = [("AR", PB)]
        phU = [("AYU", PB)]
        HK = [("hT0", kc) for kc in range(8)]

        pending = []

        def out_point(j):
            if pending:
                if j + 1 < 4:
                    pending[j + 1][0]()
                pending[j][1]()

        for t in range(4):
            for j in range(4):
                b = newbank()
                S.pe([lambda e, b=b, kc=kc, j=j: e.matmul(ps[b][:], lhsT=hT0[:, kc, j * 128:(j + 1) * 128],
                                                          rhs=w_in_sb[:, kc, V_OFF:V_OFF + 512],
                                                          start=(kc == 0), stop=(kc == 7)) for kc in range(8)],
                     reads=HK + ["w_in"], writes=[("ps", b)])
                S.op("dve", lambda e, b=b, j=j: e.bn_stats(out=bnst[:, j, :], in_=ps[b][:]),
                     reads=[("ps", b)], writes=[("bnst", j)])
                S.op("dve", lambda e, j=j: e.bn_aggr(out=bnag[:, j, 0:2], in_=bnst[:, j, :]),
                     reads=[("bnst", j)], writes=[("bnag", j)])
                S.op("act", lambda e, j=j: e.activation(out=bnag[:, j, 2:3], in_=bnag[:, j, 1:2], func=AF.Sqrt,
                                                         scale=1.0, bias=eps_sb[:, 0:1]),
                     reads=[("bnag", j), "eps"], writes=[("bnagS", j)])
                S.op("dve", lambda e, j=j: e.reciprocal(out=bnag[:, j, 2:3], in_=bnag[:, j, 2:3]),
                     reads=[("bnagS", j)], writes=[("bnag2", j)])
                S.op("dve", lambda e, j=j: e.scalar_tensor_tensor(out=bnag[:, j, 3:4], in0=bnag[:, j, 0:1], scalar=-1.0,
                                                                   in1=bnag[:, j, 2:3], op0=ALU.mult, op1=ALU.mult),
                     reads=[("bnag", j), ("bnag2", j)], writes=[("bnag3", j)])
                S.op("act", lambda e, b=b, j=j: e.activation(out=vhat[:, j, :], in_=ps[b][:], func=AF.Identity,
                                                             scale=bnag[:, j, 2:3], bias=bnag[:, j, 3:4]),
                     reads=[("ps", b), ("bnag2", j), ("bnag3", j)], writes=[("vhat", j)], phase=phB)

            def fm_group(off, fc):
                b = newbank()
                S.pe([lambda e, b=b, kc=kc: e.matmul(ps[b][:], lhsT=w_in_sb[:, kc, off + fc * 128: off + (fc + 1) * 128],
                                                     rhs=hT0[:, kc, :], start=(kc == 0), stop=(kc == 7)) for kc in range(8)],
                     reads=HK + ["w_in"], writes=[("ps", b)])
                return b

            out_point(0)
            for fc in range(4):
                b = fm_group(GB_OFF, fc)
                S.op("act", lambda e, b=b, fc=fc: e.activation(out=sgb[:, fc, :], in_=ps[b][:], func=AF.Silu),
                     reads=[("ps", b)], writes=[("sgb", fc)], phase=phB)
            out_point(1)
            for fc in range(4):
                b = fm_group(GA_OFF, fc)
                S.op("act", lambda e, b=b, fc=fc: e.activation(out=sga[:, fc, :], in_=ps[b][:], func=AF.Silu),
                     reads=[("ps", b)], writes=[("sga", fc)], phase=phB)
            out_point(2)
            for fc in range(4):
                b = fm_group(U_OFF, fc)
                S.op("dve", lambda e, b=b, fc=fc: e.tensor_tensor(out=sgb[:, fc, :], in0=ps[b][:], in1=sgb[:, fc, :], op=ALU.mult),
                     reads=[("ps", b), ("sgb", fc)], writes=[("sgb", fc)], phase=phB)
            if t + 1 < 4:
                fe1(rowsB(t + 1), FE_RING_B)
            elif u + 1 < 3 and stage == 0:
                fe1(rowsA_of(u + 1, 0), FE_RING_B)
            for fc in range(8):
                b = fm_group(MB_OFF, fc)
                S.op("act", lambda e, b=b, fc=fc: e.activation(out=smb[:, fc, :], in_=ps[b][:], func=AF.Sigmoid),
                     reads=[("ps", b)], writes=[("smb", fc)], phase=phU)
            out_point(3)
            for fc in range(8):
                b = fm_group(MA_OFF, fc)
                S.op("act", lambda e, b=b, fc=fc: e.activation(out=sma[:, fc, :], in_=ps[b][:], func=AF.Sigmoid),
                     reads=[("ps", b)], writes=[("sma", fc)], phase=phU)

            for g in range(4):
                S.op("pool", lambda e, g=g, t=t: e.tensor_tensor(out=sga[:, g, :], in0=yaT(g)[:, t * 512:(t + 1) * 512],
                                                                 in1=sga[:, g, :], op=ALU.mult),
                     reads=[("Y", g), ("sga", g)], writes=[("sga", g)], phase=phB + phU)

            if t + 1 < 4:
                fe2(hT0, "hT0", u)
            elif u + 1 < 3 and stage == 0:
                fe2(hT0, "hT0", u + 1)
                prefetched[u + 1] = True

            for h in range(4):
                b = newbank()
                S.pe([lambda e, b=b, h=h, j=j: e.matmul(ps[b][:, j * 128:(j + 1) * 128], lhsT=vhat[:, j, h * 128:(h + 1) * 128],
                                                        rhs=wsT[:, h, :], start=True, stop=True) for j in range(4)],
                     reads=[("vhat", j) for j in range(4)] + ["wsT"], writes=[("ps", b)], phase=phB)
                S.op("dve", lambda e, b=b, h=h: e.scalar_tensor_tensor(
                    out=tyb[:, h, :].rearrange("p (j q) -> p j q", j=4), in0=ps[b][:].rearrange("p (j q) -> p j q", j=4),
                    scalar=lng_sb[:, h:h + 1], in1=Ktab[:, h, :].unsqueeze(1).to_broadcast([128, 4, 128]),
                    op0=ALU.mult, op1=ALU.add),
                    reads=[("ps", b), "Ktab", "tabs"], writes=[("tyb", h)], phase=phB)
                S.op("pool", lambda e, h=h: e.tensor_tensor(out=tyb[:, h, :], in0=tyb[:, h, :], in1=sgb[:, h, :], op=ALU.mult),
                     reads=[("tyb", h), ("sgb", h)], writes=[("tyb", h)], phase=phB)

            for dmc in range(8):
                b = newbank()
                S.pe([lambda e, b=b, k=k, dmc=dmc: e.matmul(ps[b][:], lhsT=w_pa_sb[:, k, dmc * 128:(dmc + 1) * 128],
                                                            rhs=sga[:, k, :], start=(k == 0), stop=(k == 3)) for k in range(4)],
                     reads=[("sga", k) for k in range(4)] + ["w_pab"], writes=[("ps", b)], phase=phB)
                S.op("dve", lambda e, b=b, dmc=dmc: e.tensor_tensor(out=sma[:, dmc, :], in0=ps[b][:], in1=sma[:, dmc, :], op=ALU.mult),
                     reads=[("ps", b), ("sma", dmc)], writes=[("sma", dmc)], phase=phU)
            for dmc in range(8):
                b = newbank()
                S.pe([lambda e, b=b, k=k, dmc=dmc: e.matmul(ps[b][:], lhsT=w_pb_sb[:, k, dmc * 128:(dmc + 1) * 128],
                                                            rhs=tyb[:, k, :], start=(k == 0), stop=(k == 3)) for k in range(4)],
                     reads=[("tyb", k) for k in range(4)] + ["w_pab"], writes=[("ps", b)], phase=phB)
                S.op("dve", lambda e, b=b, dmc=dmc: e.tensor_tensor(out=smb[:, dmc, :], in0=ps[b][:], in1=smb[:, dmc, :], op=ALU.mult),
                     reads=[("ps", b), ("smb", dmc)], writes=[("smb", dmc)], phase=phU)
                S.op("pool", lambda e, dmc=dmc: e.tensor_tensor(out=sma[:, dmc, :], in0=sma[:, dmc, :], in1=smb[:, dmc, :], op=ALU.add),
                     reads=[("sma", dmc), ("smb", dmc)], writes=[("sma", dmc)], phase=phU)

            def make_chunk(t, j):
                st = {}
                row0 = u * UT + t * 512 + j * 128

                def load():
                    k = BE_RING[be_ctr[0] % len(BE_RING)]
                    be_ctr[0] += 1
                    st["k"] = k
                    S.dma("sp", [lambda e, k=k: e.dma_start(out=xy[:, k, :], in_=xu.ap()[row0:row0 + 128, :])],
                          "s_ld%d" % k, writes=[("xy", k)])

                def chunk(junk=junk):
                    k = st["k"]
                    si = st_ctr[0] % 8
                    st_ctr[0] += 1
                    for half in range(2):
                        b = newbank()
                        S.pe([lambda e, b=b, dmc=dmc, half=half: e.matmul(
                            ps[b][:], lhsT=sma[:, dmc, j * 128:(j + 1) * 128], rhs=w_out_sb[:, dmc, half * 512:(half + 1) * 512],
                            start=(dmc == 0), stop=(dmc == 7)) for dmc in range(8)],
                            reads=[("sma", dmc) for dmc in range(8)] + ["w_out"], writes=[("ps", b)], phase=phU)
                        S.op("dve", lambda e, b=b, k=k, half=half: e.tensor_tensor(
                            out=xy[:, k, half * 512:(half + 1) * 512], in0=ps[b][:], in1=xy[:, k, half * 512:(half + 1) * 512],
                            op=ALU.add),
                            reads=[("ps", b), ("xy", k)], writes=[("xy", k)])
                    S.op("act", lambda e, k=k, si=si: e.activation(out=junk, in_=xy[:, k, :], func=AF.Square,
                                                                   accum_out=stat[:, si, 0:1]),
                         reads=[("xy", k)], writes=[("st", si), "junk"], phase=phU)
                    S.op("act", lambda e, si=si: e.activation(out=stat[:, si, 1:2], in_=stat[:, si, 0:1], func=AF.Sqrt,
                                                               scale=1.0 / D, bias=eps_sb[:, 0:1]),
                         reads=[("st", si), "eps"], writes=[("st1", si)])
                    S.op("dve", lambda e, si=si: e.reciprocal(out=stat[:, si, 2:3], in_=stat[:, si, 1:2]),
                         reads=[("st1", si)], writes=[("st2", si)])
                    S.op("dve", lambda e, k=k, si=si: e.scalar_tensor_tensor(out=xy[:, k, :], in0=xy[:, k, :],
                                                                              scalar=stat[:, si, 2:3], in1=fg_bc[:],
                                                                              op0=ALU.mult, op1=ALU.mult),
                         reads=[("xy", k), ("st2", si), "tabs"], writes=[("xy", k)])
                    tok = S.dma("sp", [lambda e, k=k: e.dma_start(out=yu.ap()[row0:row0 + 128, :], in_=xy[:, k, :])],
                                "s_st%d" % k, reads=[("xy", k)], writes=[("yout", row0)])
                    out_tokens.append(tok)
                return load, chunk

            pending[:] = [make_chunk(t, j) for j in range(4)]
            pending[0][0]()

        if u + 1 < 3 and stage == 0:
            pending_x[:] = list(pending)
        else:
            for j in range(4):
                if j + 1 < 4:
                    pending[j + 1][0]()
                pending[j][1]()
        pending[:] = []

    for u in range(3):
        do_unit(u)
        if stage in (2, 3, 4, 21, 22, 23, 31, 32, 33, 34, 35):
            break

    last = {}
    for sk, v in out_tokens:
        last[sk] = max(last.get(sk, 0), v)
    for sk, v in last.items():
        S.streams["sp"].append(("wait", sk, v))

    with nc.Block() as block:
        def run(stream, e):
            for it in stream:
                if it[0] == "wait":
                    e.wait_ge(sems[it[1]], it[2])
                elif it[0] == "raw":
                    it[1](e)
                else:
                    ins = it[1](e)
                    ins.then_inc(sems[it[2]], it[3])

        @block.tensor
        def _(e):
            run(S.streams["pe"], e)

        @block.scalar
        def _(e):
            run(S.streams["act"], e)

        @block.vector
        def _(e):
            run(S.streams["dve"], e)

        @block.gpsimd
        def _(e):
            run(S.streams["pool"], e)

        @block.sync
        def _(e):
            run(S.streams["sp"], e)

    es.close()
    return nc


_NC_CACHE = {}
_STAGE = 0


def kernel(x_prompt, x_sample, c_prompt, c_sample, norm_g, w_ada, b_ada, w_in, w_fmix, b_fmix,
           sgu_ln_g, sgu_ln_b, w_s, b_s, w_pa, w_pb, w_out, final_g):
    f = lambda a: np.ascontiguousarray(np.asarray(a, dtype=np.float32))
    x_prompt, x_sample, c_prompt, c_sample = f(x_prompt), f(x_sample), f(c_prompt), f(c_sample)
    norm_g, w_ada, b_ada, w_in = f(norm_g)[0], f(w_ada)[0], f(b_ada)[0], f(w_in)[0]
    w_fmix, b_fmix = f(w_fmix)[0], f(b_fmix)[0]
    sgu_ln_g, sgu_ln_b, w_s, b_s = f(sgu_ln_g)[0], f(sgu_ln_b)[0], f(w_s)[0], f(b_s)[0]
    w_pa, w_pb, w_out, final_g = f(w_pa)[0], f(w_pb)[0], f(w_out)[0], f(final_g)

    if "nc" not in _NC_CACHE:
        _NC_CACHE["nc"] = build_nc(_STAGE)
    nc = _NC_CACHE["nc"]

    shared = {
        "w_ada": w_ada, "w_in": w_in, "w_pa": w_pa, "w_pb": w_pb, "w_out": w_out,
        "b_adaT": f(b_ada.reshape(24, 128).T), "norm_gT": f(norm_g.reshape(8, 128).T),
        "wfm": f(w_fmix.transpose(1, 0, 2).reshape(128, 512)),
        "b_fmixT": f(b_fmix.T), "ln_gT": f(sgu_ln_g.reshape(4, 128).T), "ln_b": sgu_ln_b,
        "w_s": f(w_s.reshape(512, 128)), "b_s": f(b_s.reshape(512)), "final_g": final_g,
    }
    in_maps = []
    for k in range(NCORES):
        q, hf = k // 2, k % 2
        cst = _consts(hf)
        c3 = np.stack([c_sample[q], c_prompt[2 * k], c_prompt[2 * k + 1]], axis=0)
        cTk = f(c3.reshape(3, 8, 128).transpose(2, 1, 0).reshape(128, 24))
        xuk = np.concatenate([x_sample[q, hf * UT:(hf + 1) * UT], x_prompt[2 * k], x_prompt[2 * k + 1]], axis=0)
        m = dict(shared)
        m.update({"xs_full": x_sample[q], "xu": f(xuk), "cT": cTk})
        m.update(cst)
        in_maps.append(m)

    res = run_bass_kernel_spmd(nc, in_maps, core_ids=list(range(NCORES)))
    y_prompt = np.empty_like(x_prompt)
    y_sample = np.empty_like(x_sample)
    for k in range(NCORES):
        q, hf = k // 2, k % 2
        yu = np.asarray(res.results[k]["yu"], dtype=np.float32)
        y_sample[q, hf * UT:(hf + 1) * UT] = yu[0:UT]
        y_prompt[2 * k] = yu[UT:2 * UT]
        y_prompt[2 * k + 1] = yu[2 * UT:3 * UT]
    return (y_prompt, y_sample)
```

```python
import numpy as np
import ml_dtypes
import concourse.bass as bass
import concourse.mybir as mybir
from concourse.bass_utils import run_bass_kernel_spmd

F32 = mybir.dt.float32
BF16 = mybir.dt.bfloat16
AF = mybir.ActivationFunctionType
ALU = mybir.AluOpType

D = 1024
INW = 4608
EPS = 1e-6
NCORES = 8
UT = 2048
A_OFF, GA_OFF, U_OFF, V_OFF, GB_OFF, MA_OFF, MB_OFF = 0, 512, 1024, 1536, 2048, 2560, 3584


class Sched:
    ENGS = ("pe", "act", "dve", "pool", "sp")

    def __init__(self):
        self.streams = {e: [] for e in self.ENGS}
        self.count = {}
        self.waited = {e: {} for e in self.ENGS}
        self.last_w = {}
        self.readers = {}
        self.phase_users = {}

    def _waits(self, eng, reads, writes, phase, extra):
        deps = {}

        def add(tok, raw):
            sk, v = tok
            if sk == eng and (eng == "pe" or not raw):
                return
            if deps.get(sk, 0) < v:
                deps[sk] = v

        for b in reads:
            t = self.last_w.get(b)
            if t:
                add(t, True)
        for b in writes:
            t = self.last_w.get(b)
            if t:
                add(t, False)
            for r in self.readers.get(b, ()):
                add(r, False)
        for t in extra:
            add(t, True)
        for (res, ph) in phase:
            prev = self.phase_users.get((res, ph - 1))
            if prev:
                for sk, v in prev.items():
                    add((sk, v), False)
        for sk, v in deps.items():
            if self.waited[eng].get(sk, 0) < v:
                self.streams[eng].append(("wait", sk, v))
                self.waited[eng][sk] = v

    def _update(self, reads, writes, phase, tok):
        for b in reads:
            self.readers.setdefault(b, []).append(tok)
        for b in writes:
            self.last_w[b] = tok
            self.readers[b] = []
        for key in phase:
            d = self.phase_users.setdefault(key, {})
            if d.get(tok[0], 0) < tok[1]:
                d[tok[0]] = tok[1]

    def op(self, eng, fn, reads=(), writes=(), phase=(), extra=()):
        self._waits(eng, reads, writes, phase, extra)
        self.count[eng] = self.count.get(eng, 0) + 1
        tok = (eng, self.count[eng])
        self.streams[eng].append(("op", fn, eng, 1))
        self._update(reads, writes, phase, tok)
        return tok

    def pe(self, fns, reads=(), writes=(), phase=(), extra=()):
        self._waits("pe", reads, writes, phase, extra)
        for f in fns[:-1]:
            self.streams["pe"].append(("raw", f))
        self.count["pe"] = self.count.get("pe", 0) + 1
        tok = ("pe", self.count["pe"])
        self.streams["pe"].append(("op", fns[-1], "pe", 1))
        self._update(reads, writes, phase, tok)
        return tok

    def dma(self, queue, fns, sem, reads=(), writes=(), phase=(), extra=()):
        self._waits(queue, reads, writes, phase, extra)
        for f in fns:
            self.count[sem] = self.count.get(sem, 0) + 16
            self.streams[queue].append(("op", f, sem, 16))
        tok = (sem, self.count[sem])
        self._update(reads, writes, phase, tok)
        return tok


def _consts(hf):
    c = {}
    c["identb"] = np.eye(128, dtype=np.float32).astype(ml_dtypes.bfloat16)
    c["identf"] = np.eye(128, dtype=np.float32)
    c["onesf"] = np.ones((128, 128), dtype=np.float32)
    cc = np.arange(128)
    ph = 2.0 * np.pi * np.outer(cc, cc) / 128.0
    c["cosC"] = np.cos(ph).astype(np.float32)
    c["sinC"] = np.sin(ph).astype(np.float32)
    r2 = np.zeros((128, 256), dtype=np.float64)
    for s1j in range(8):
        for s2 in range(16):
            p = 8 * s2 + s1j
            for hh in range(2):
                for k2l in range(8):
                    k2 = 8 * hh + k2l
                    ang = 2.0 * np.pi * s2 * k2 / 16.0
                    r2[p, hh * 128 + 0 * 64 + k2l * 8 + s1j] = np.cos(ang)
                    r2[p, hh * 128 + 1 * 64 + k2l * 8 + s1j] = -np.sin(ang)
    c["r2h"] = r2.astype(np.float32)
    for u, (S, off) in enumerate(((4096, 128 * hf), (2048, 0))):
        N1 = S // 16
        KC = N1 // 128
        scale = 1.0 / np.sqrt(S * 128.0)
        s1 = (128 * np.arange(KC)[None, :] + np.arange(128)[:, None]).astype(np.float64)
        k = 16.0 * (np.arange(128) + off)[None, :] + np.arange(16)[:, None]
        th = 2.0 * np.pi * s1[None, :, :, None] * k[:, None, None, :] / S
        E = np.stack([np.cos(th), np.sin(th)], axis=3) * scale
        c["Ek%d" % u] = E.reshape(16 * 128, KC * 2 * 128).astype(np.float32)
    return c


_CONST_SHAPES = None


def build_nc(stage=0):
    nc = bass.Bass("TRN2", target_bir_lowering=False)
    S = Sched()

    def din(name, shape, dt=F32):
        return nc.dram_tensor(name, list(shape), dt, kind="ExternalInput")

    xs_full = din("xs_full", [4096, D])
    xu = din("xu", [3 * UT, D])
    cT = din("cT", [128, 24])
    w_ada = din("w_ada", [D, 3 * D])
    b_adaT = din("b_adaT", [128, 24])
    norm_gT = din("norm_gT", [128, 8])
    w_in = din("w_in", [D, INW])
    w_pa = din("w_pa", [512, D])
    w_pb = din("w_pb", [512, D])
    w_out = din("w_out", [D, D])
    wfm = din("wfm", [128, 512])
    b_fmixT = din("b_fmixT", [128, 4])
    ln_gT = din("ln_gT", [128, 4])
    ln_b = din("ln_b", [512])
    w_s = din("w_s", [512, 128])
    b_s = din("b_s", [512])
    final_g = din("final_g", [D])
    identb = din("identb", [128, 128], BF16)
    identf = din("identf", [128, 128])
    onesf = din("onesf", [128, 128])
    cosC = din("cosC", [128, 128])
    sinC = din("sinC", [128, 128])
    r2h = din("r2h", [128, 256])
    Ek0 = din("Ek0", [16 * 128, 2 * 2 * 128])
    Ek1 = din("Ek1", [16 * 128, 1 * 2 * 128])
    yu = nc.dram_tensor("yu", [3 * UT, D], F32, kind="ExternalOutput")

    from contextlib import ExitStack
    es = ExitStack()

    def sb(name, shape, dt):
        return es.enter_context(nc.sbuf_tensor(name, list(shape), dt))

    w_in_sb = sb("w_in_sb", [128, 8, INW], BF16)
    w_pa_sb = sb("w_pa_sb", [128, 4, D], BF16)
    w_pb_sb = sb("w_pb_sb", [128, 4, D], BF16)
    w_out_sb = sb("w_out_sb", [128, 8, D], BF16)
    AY = sb("AY", [128, 16384], BF16)
    Y0 = sb("Y0", [128, 2048], BF16)
    AR = sb("AR", [128, 8192], BF16)
    hT0 = sb("hT0", [128, 8, 512], BF16)
    xsr = sb("xsr", [128, 4, D], BF16)
    NXY = 4
    xy = sb("xy", [128, NXY, D], F32)
    gate_bc = sb("gate_bc", [128, D], F32)
    fg_bc = sb("fg_bc", [128, D], F32)
    csw = sb("csw", [128, 4, 384], BF16)
    identb_sb = sb("identb_sb", [128, 128], BF16)
    identf_sb = sb("identf_sb", [128, 128], F32)
    onesf_sb = sb("onesf_sb", [128, 128], F32)
    dgt = sb("dgt", [128, 2, 128], F32)
    r2_sb = sb("r2_sb", [128, 256], BF16)
    wsT = sb("wsT", [128, 4, 128], BF16)
    Ktab = sb("Ktab", [128, 4, 128], BF16)
    cT_sb = sb("cT_sb", [128, 24], F32)
    cs_sb = sb("cs_sb", [128, 24], F32)
    badaT_sb = sb("badaT_sb", [128, 24], F32)
    ngT_sb = sb("ngT_sb", [128, 8], F32)
    modT = sb("modT", [128, 24, 3], F32)
    Gtab = sb("Gtab", [128, 8, 3], F32)
    Gtmp = sb("Gtmp", [128, 8, 3], F32)
    bfm_sb = sb("bfm_sb", [128, 4], F32)
    lng_sb = sb("lng_sb", [128, 4], F32)
    ARf = AR[:, 0:3584].bitcast(F32)
    cosC_sb = ARf[:, 0:128]
    sinC_sb = ARf[:, 128:256]
    wfm_sb = ARf[:, 256:768]
    lnb_f = ARf[:, 768:1280]
    bs_bc = ARf[:, 1280:1792]
    lnb_b = AR[:, 3584:4096]
    ws_nat = AR[:, 4096:4608].rearrange("p (h q) -> p h q", h=4)
    junk = AY[:, 6144:7168]
    eps_sb = sb("eps_sb", [128, 1], F32)
    stat = sb("stat", [128, 8, 4], F32)
    bnst = sb("bnst", [128, 4, 6], F32)
    bnag = sb("bnag", [128, 4, 4], F32)
    ps = [es.enter_context(nc.psum_tensor("ps%d" % i, [128, 512], F32)) for i in range(8)]

    semnames = ["pe", "act", "dve", "pool", "s_win", "s_wina", "s_wrest", "s_wpab", "s_tab", "s_wada0", "s_wada1",
                "s_ld0", "s_ld1", "s_ld2", "s_ld3", "s_ld4", "s_ld5", "s_ld6", "s_ek0", "s_ek1", "s_ek2", "s_ek3", "s_ek4", "s_ek5", "s_ek6", "s_ek7", "s_st0", "s_st1", "s_st2", "s_st3", "s_wout"]
    sems = {n: es.enter_context(nc.semaphore(n)) for n in semnames}

    bank_ctr = [0]
    bank_ring = list(range(8))

    def newbank():
        b = bank_ring[bank_ctr[0] % len(bank_ring)]
        bank_ctr[0] += 1
        return b

    def psb(b):
        return ps[b][:].bitcast(BF16)

    def dram_ap(t, offset, dims):
        return bass.AP(t.ap().tensor, offset, [list(d) for d in dims])

    tabs = []

    def tab(dst_ap, src_ap):
        tabs.append(lambda e, d=dst_ap, s=src_ap: e.dma_start(out=d, in_=s))

    tab(cT_sb[:], cT.ap())
    tab(badaT_sb[:], b_adaT.ap())
    tab(ngT_sb[:], norm_gT.ap())
    tab(identf_sb[:], identf.ap())
    tab(onesf_sb[:], onesf.ap())
    tab(identb_sb[:], identb.ap())
    tab(cosC_sb, cosC.ap())
    tab(sinC_sb, sinC.ap())
    tab(wfm_sb, wfm.ap())
    tab(bfm_sb[:], b_fmixT.ap())
    tab(lng_sb[:], ln_gT.ap())
    tab(lnb_f, dram_ap(ln_b, 0, [[0, 128], [1, 512]]))
    tab(bs_bc, dram_ap(b_s, 0, [[0, 128], [1, 512]]))
    tab(fg_bc[:], dram_ap(final_g, 0, [[0, 128], [1, D]]))
    TABKEYS = ["tabs"]
    S.dma("sp", tabs, "s_tab", writes=TABKEYS, phase=[("AR", 0)])

    AYf = AY[:].bitcast(F32)
    wada_blk = [AYf[:, 0:4096].rearrange("p (k n) -> p k n", k=8),
                AYf[:, 4096:8192].rearrange("p (k n) -> p k n", k=8)]

    def wada_dma(cb):
        src = dram_ap(w_ada, cb * 512, [[3 * D, 128], [128 * 3 * D, 8], [1, 512]])
        return lambda e, s=src, d=wada_blk[cb % 2]: e.dma_start(out=d, in_=s)

    wla, wl = [], []
    for kc in range(8):
        wla.append(lambda e, kc=kc: e.dma_start(out=w_in_sb[:, kc, 0:512], in_=w_in.ap()[kc * 128:(kc + 1) * 128, 0:512]))
    for cb in range(2):
        for kc in range(8):
            wl.append(lambda e, kc=kc, cb=cb: e.dma_start(
                out=w_in_sb[:, kc, 512 + cb * 2048: 512 + (cb + 1) * 2048],
                in_=w_in.ap()[kc * 128:(kc + 1) * 128, 512 + cb * 2048: 512 + (cb + 1) * 2048]))
    wr = []
    wpab = []
    for k in range(4):
        wpab.append(lambda e, k=k: e.dma_start(out=w_pa_sb[:, k, :], in_=w_pa.ap()[k * 128:(k + 1) * 128, :]))
        wpab.append(lambda e, k=k: e.dma_start(out=w_pb_sb[:, k, :], in_=w_pb.ap()[k * 128:(k + 1) * 128, :]))
    wr.append(lambda e: e.dma_start(out=r2_sb[:], in_=r2h.ap()))
    wr.append(lambda e: e.dma_start(out=ws_nat, in_=dram_ap(w_s, 0, [[128, 128], [128 * 128, 4], [1, 128]])))

    S.op("dve", lambda e: e.memset(eps_sb[:], EPS), writes=["eps"])
    S.op("act", lambda e: e.activation(out=cs_sb[:], in_=cT_sb[:], func=AF.Silu),
         reads=TABKEYS, writes=["cs"])
    bmod = newbank()
    for cb in range(6):
        S.dma("act", [wada_dma(cb)], "s_wada%d" % (cb % 2), writes=[("wada", cb % 2)], phase=[("AYU", 0)])
        fns = []
        for ch in range(4):
            chunk = cb * 4 + ch
            for kc in range(8):
                fns.append(lambda e, chunk=chunk, ch=ch, kc=kc, cb=cb: e.matmul(
                    ps[bmod][:, chunk * 3:chunk * 3 + 3],
                    lhsT=wada_blk[cb % 2][:, kc, ch * 128:(ch + 1) * 128],
                    rhs=cs_sb[:, kc * 3:kc * 3 + 3],
                    start=(kc == 0), stop=(kc == 7)))
        S.pe(fns, reads=[("wada", cb % 2), "cs"], writes=[("ps", bmod)], phase=[("AYU", 0)])
    S.dma("pool", wla, "s_wina", writes=["w_in_a"])
    S.dma("pool", wr, "s_wrest", writes=["w_rest"], phase=[("AR", 0)])

    S.op("dve", lambda e: e.tensor_tensor(out=modT[:], in0=ps[bmod][:, 0:72].rearrange("p (c b) -> p c b", b=3),
                                          in1=badaT_sb[:].unsqueeze(2).to_broadcast([128, 24, 3]), op=ALU.add),
         reads=[("ps", bmod), "tabs"], writes=["modT"])
    S.op("dve", lambda e: e.tensor_scalar(out=Gtmp[:], in0=modT[:, 8:16, :], scalar1=1.0, scalar2=None, op0=ALU.add),
         reads=["modT"], writes=["Gtmp"])
    S.op("dve", lambda e: e.tensor_tensor(out=Gtab[:], in0=Gtmp[:],
                                          in1=ngT_sb[:].unsqueeze(2).to_broadcast([128, 8, 3]), op=ALU.mult),
         reads=["Gtmp", "tabs"], writes=["Gtab"])

    P0 = [("AR", 0)]
    for g in range(4):
        b = newbank()
        S.pe([lambda e, g=g, b=b: e.matmul(ps[b][:, 0:128], lhsT=cosC_sb, rhs=wfm_sb[:, g * 128:(g + 1) * 128],
                                           start=True, stop=True),
              lambda e, g=g, b=b: e.matmul(ps[b][:, 128:256], lhsT=sinC_sb, rhs=wfm_sb[:, g * 128:(g + 1) * 128],
                                           start=True, stop=True)],
             reads=["tabs"], writes=[("ps", b)], phase=P0)
        S.op("act", lambda e, g=g, b=b: e.activation(out=csw[:, g, 128:256], in_=ps[b][:, 0:128], func=AF.Copy),
             reads=[("ps", b)], writes=[("csw", g)])
        S.op("act", lambda e, g=g, b=b: e.activation(out=csw[:, g, 256:384], in_=ps[b][:, 128:256], func=AF.Copy, scale=-1.0),
             reads=[("ps", b)], writes=[("csw", g)])
        S.op("act", lambda e, g=g, b=b: e.activation(out=csw[:, g, 0:128], in_=ps[b][:, 128:256], func=AF.Copy),
             reads=[("ps", b)], writes=[("csw", g)])

    bt = newbank()
    S.pe([lambda e, h=h: e.transpose(psb(bt)[:, h * 128:(h + 1) * 128], ws_nat[:, h, :], identb_sb[:]) for h in range(4)],
         reads=["w_rest", "tabs"], writes=[("ps", bt)], phase=P0)
    S.op("act", lambda e: e.activation(out=wsT[:].rearrange("p h q -> p (h q)"), in_=psb(bt)[:, 0:512], func=AF.Copy),
         reads=[("ps", bt)], writes=["wsT"])
    S.op("dve", lambda e: e.tensor_copy(out=lnb_b, in_=lnb_f), reads=["tabs"], writes=["lnb_b"], phase=P0)
    bk = newbank()
    S.pe([lambda e, h=h: e.matmul(ps[bk][:, h * 128:(h + 1) * 128], lhsT=lnb_b[:, h * 128:(h + 1) * 128], rhs=wsT[:, h, :],
                                  start=True, stop=True) for h in range(4)],
         reads=["lnb_b", "wsT"], writes=[("ps", bk)], phase=P0)
    S.op("dve", lambda e: e.tensor_tensor(out=Ktab[:].rearrange("p h q -> p (h q)"), in0=ps[bk][:, 0:512], in1=bs_bc, op=ALU.add),
         reads=[("ps", bk), "tabs"], writes=["Ktab"], phase=P0)

    fe_ctr = [0]
    be_ctr = [0]
    st_ctr = [0]
    FE_RING_A = [0, 1, 4, 5, 6]
    FE_RING_B = [0, 1]
    BE_RING = [2, 3]

    def xys(k):
        if k < NXY:
            return xy[:, k, :]
        if k == 6:
            return Y0[:].bitcast(F32)
        return AR[:, 4096 + 2048 * (k - 4): 4096 + 2048 * (k - 3)].bitcast(F32)

    def fe1(row_aps, ring, phase=()):
        for j, src in enumerate(row_aps):
            k = ring[fe_ctr[0] % len(ring)]
            fe_ctr[0] += 1
            si = st_ctr[0] % 8
            st_ctr[0] += 1
            ph = phase if k in (4, 5) else ()
            xk = [("xy", k)] + ([("Y", 0)] if k == 6 else [])
            S.dma("sp", [lambda e, k=k, src=src: e.dma_start(out=xys(k), in_=src)], "s_ld%d" % k,
                  writes=xk, phase=ph)
            S.op("act", lambda e, k=k, j=j, si=si: e.activation(out=xsr[:, j, :], in_=xys(k), func=AF.Square,
                                                                 accum_out=stat[:, si, 0:1]),
                 reads=xk, writes=[("xs", j), ("st", si)], phase=ph)
            S.op("act", lambda e, si=si: e.activation(out=stat[:, si, 1:2], in_=stat[:, si, 0:1], func=AF.Sqrt,
                                                       scale=1.0 / D, bias=eps_sb[:, 0:1]),
                 reads=[("st", si), "eps"], writes=[("st1", si)])
            S.op("dve", lambda e, si=si: e.reciprocal(out=stat[:, si, 2:3], in_=stat[:, si, 1:2]),
                 reads=[("st1", si)], writes=[("st2", si)])
            S.op("dve", lambda e, k=k, j=j, si=si: e.tensor_scalar(out=xsr[:, j, :], in0=xys(k),
                                                                   scalar1=stat[:, si, 2:3], scalar2=None, op0=ALU.mult),
                 reads=xk + [("st2", si)], writes=[("xs", j)], phase=ph)

    def fe2(hT, hkey, slot, phase=()):
        for kp in range(4):
            b = newbank()
            fns = []
            for kk in range(2):
                kc = 2 * kp + kk
                for j in range(4):
                    fns.append(lambda e, b=b, kk=kk, kc=kc, j=j: e.transpose(
                        psb(b)[:, kk * 512 + j * 128: kk * 512 + (j + 1) * 128],
                        xsr[:, j, kc * 128:(kc + 1) * 128], identb_sb[:]))
            S.pe(fns, reads=[("xs", j) for j in range(4)] + ["tabs"], writes=[("ps", b)])
            for kk in range(2):
                kc = 2 * kp + kk
                S.op("dve", lambda e, b=b, kk=kk, kc=kc: e.tensor_scalar(
                    out=hT[:, kc, :], in0=psb(b)[:, kk * 512:(kk + 1) * 512],
                    scalar1=Gtab[:, kc, slot:slot + 1], scalar2=modT[:, kc, slot:slot + 1],
                    op0=ALU.mult, op1=ALU.add),
                    reads=[("ps", b), "Gtab", "modT"], writes=[(hkey, kc)], phase=phase)

    hT1 = AR[:, 0:4096].rearrange("p (k n) -> p k n", k=8)
    hTs = [hT0, hT1]

    out_tokens = []

    def rowsA_of(uu, s):
        Sin_ = 4096 if uu == 0 else 2048
        N1_ = Sin_ // 16
        xsrc_ = xs_full if uu == 0 else xu
        xbase_ = 0 if uu == 0 else uu * UT
        return [dram_ap(xsrc_, (xbase_ + 8 * (4 * s + j)) * D, [[N1_ * D, 16], [D, 8], [1, D]]) for j in range(4)]

    prefetched = {}
    pending_x = []

    def do_unit(u):
        if stage == 1:
            return
        Sin = 4096 if u == 0 else 2048
        N1 = Sin // 16
        KC = N1 // 128
        NT = Sin // 128
        NS = NT // 4
        Ek_d = Ek0 if u == 0 else Ek1
        xsrc = xs_full if u == 0 else xu
        xbase = 0 if u == 0 else u * UT
        PA, PF, PB = 3 * u + 1, 3 * u + 2, 3 * u + 3
        A_tm = AY[:, 0:4 * NT * 128].rearrange("p (g b c) -> p g b c", g=4, b=NT)

        def yaT(g):
            return Y0[:, :] if g == 0 else AY[:, (g - 1) * 2048: g * 2048]

        def gate_pre(kc):
            S.op("dve", lambda e, kc=kc: e.tensor_scalar(out=dgt[:, kc % 2, :], in0=identf_sb[:],
                                                        scalar1=modT[:, 16 + kc, u:u + 1], scalar2=None, op0=ALU.mult),
                 reads=["modT", "tabs"], writes=[("dgt", kc % 2)])

        def gate_post(kc):
            b = newbank()
            S.pe([lambda e, b=b, kc=kc: e.matmul(ps[b][:, 0:128], lhsT=onesf_sb[:], rhs=dgt[:, kc % 2, :], start=True, stop=True)],
                 reads=[("dgt", kc % 2), "tabs"], writes=[("ps", b)])
            S.op("act", lambda e, b=b, kc=kc: e.activation(out=gate_bc[:, kc * 128:(kc + 1) * 128], in_=ps[b][:, 0:128], func=AF.Copy),
                 reads=[("ps", b)], writes=["gate_bc"])

        def rowsA(s):
            aps = []
            for j in range(4):
                s1b = 4 * s + j
                aps.append(dram_ap(xsrc, (xbase + 8 * s1b) * D, [[N1 * D, 16], [D, 8], [1, D]]))
            return aps

        def passA_mm(s):
            hT = hTs[s % 2]
            hkey = "hT%d" % (s % 2)
            for j in range(4):
                s1b = 4 * s + j
                b = newbank()
                S.pe([lambda e, b=b, kc=kc, j=j, hT=hT: e.matmul(ps[b][:], lhsT=hT[:, kc, j * 128:(j + 1) * 128],
                                                                  rhs=w_in_sb[:, kc, A_OFF:A_OFF + 512],
                                                                  start=(kc == 0), stop=(kc == 7)) for kc in range(8)],
                     reads=[(hkey, kc) for kc in range(8)] + ["w_in_a"], writes=[("ps", b)],
                     phase=[("AR", PA)] if s % 2 == 1 else ())
                S.op("act", lambda e, b=b, s1b=s1b: e.activation(out=A_tm[:, :, s1b, :],
                                                                 in_=ps[b][:].rearrange("p (g c) -> p g c", g=4), func=AF.Copy),
                     reads=[("ps", b)], writes=[("A", g) for g in range(4)] + [("Y", g) for g in range(1, 4)],
                     phase=[("AYU", PA)])

        def rowsB(t):
            return [xu.ap()[u * UT + t * 512 + j * 128: u * UT + t * 512 + (j + 1) * 128, :] for j in range(4)]

        def fe2A(s):
            fe2(hTs[s % 2], "hT%d" % (s % 2), u, phase=[("AR", PA)] if s % 2 == 1 else ())

        if stage == 21:
            return
        if not prefetched.get(u):
            fe1(rowsA(0), FE_RING_A, [("AR", PA)])
        if stage == 22:
            return
        if not prefetched.get(u):
            fe2A(0)
        if stage == 23:
            return
        if NS > 1:
            fe1(rowsA(1), FE_RING_A, [("AR", PA)])
        gsteps = 8 // NS
        for s in range(NS):
            for q in range(gsteps):
                gate_pre(s * gsteps + q)
            if s + 1 < NS:
                fe2A(s + 1)
            else:
                fe2(hT0, "hT0", u)
            if s + 2 < NS:
                fe1(rowsA(s + 2), FE_RING_A, [("AR", PA)])
            elif s + 2 == NS:
                fe1(rowsB(0), FE_RING_A, [("AR", PA)])
            passA_mm(s)
            for q in range(gsteps):
                gate_post(s * gsteps + q)
            if pending_x and s < 4:
                if s + 1 < 4:
                    pending_x[s + 1][0]()
                pending_x[s][1](AY[:, 12288:13312])
                if s == 3:
                    pending_x[:] = []
        nslot = 4 if KC == 2 else 2
        xyb = xy[:, 0:nslot, :].rearrange("p s d -> p (s d)").bitcast(BF16)
        Ekr = xyb[:, 0:16 * KC * 256].rearrange("p (r k i d) -> p r k i d", r=16, k=KC, i=2)
        EKK = ["Ek"] + [("xy", s_) for s_ in range(nslot)]
        S.dma("pool", [lambda e, q=q: e.dma_start(
            out=Ekr[:, 4 * q:4 * q + 4, :, :, :].rearrange("p r k i d -> p r (k i d)"),
            in_=dram_ap(Ek_d, 4 * q * 128 * KC * 256, [[KC * 256, 128], [128 * KC * 256, 4], [1, KC * 256]]))
            for q in range(4)], "s_ek0", writes=EKK)

        S.dma("pool", [lambda e, k=k: e.dma_start(out=w_out_sb[:, k, :], in_=w_out.ap()[k * 128:(k + 1) * 128, :])
                       for k in range(8)], "s_wout", writes=["w_out"])
        for k in range(8):
            S.op("pool", lambda e, k=k: e.tensor_tensor(out=w_out_sb[:, k, :], in0=w_out_sb[:, k, :], in1=gate_bc[:], op=ALU.mult),
                 reads=["w_out", "gate_bc"], writes=["w_out"])
        if u == 0:
            S.dma("pool", wl, "s_win", writes=["w_in"])
            S.dma("pool", wpab, "s_wpab", writes=["w_pab"])

        if stage == 2:
            return
        if stage == 31:
            return

        combos_gh_pre = [(g, hh) for g in range(4) for hh in range(2)]
        NG = 2
        if KC == 1:
            Gts = [AR[:, i * 16 * N1:(i + 1) * 16 * N1].rearrange("p (q s) -> p q s", q=16) for i in range(NG)]
            gkeys = [[("G", 0)], [("G", 1)]]
        else:
            Gts = [AR[:, 0:16 * N1].rearrange("p (q s) -> p q s", q=16),
                   xsr[:].rearrange("p j d -> p (j d)").rearrange("p (q s) -> p q s", q=16)]
            gkeys = [[("G", 0)], [("xs", j) for j in range(4)]]
        NR = 8
        Hq = AR[:, 4096:4096 + NR * KC * 256].rearrange("p (r k i d) -> p r k i d", r=NR, k=KC, i=2)
        def ekkeys(r):
            return EKK

        ev_ctr = [0]
        h_ctr = [0]
        PFA = [("AR", PF)]
        combos_gh = [(g, hh) for g in range(4) for hh in range(2)]

        def s2stage(ci):
            g, hh = combos_gh[ci]
            Gt = Gts[ci % NG]
            gkey = gkeys[ci % NG]
            for sb4 in range(NT // 4):
                b = newbank()
                S.pe([lambda e, b=b, bb=bb, sb4=sb4: e.matmul(
                    ps[b][:, bb * 128:(bb + 1) * 128], lhsT=A_tm[:, g, 4 * sb4 + bb, :],
                    rhs=r2_sb[:, hh * 128:(hh + 1) * 128], start=True, stop=True) for bb in range(4)],
                    reads=[("A", g), "w_rest"], writes=[("ps", b)], phase=[("AYU", PF)])
                src_ = ps[b][:].rearrange("p (b q j) -> p b q j", b=4, q=16)
                dst_ = Gt[:, :, 32 * sb4: 32 * sb4 + 32].rearrange("p q (b j) -> p b q j", b=4)
                if ev_ctr[0] % 4 == 0:
                    S.op("act", lambda e, src_=src_, dst_=dst_: e.activation(out=dst_, in_=src_, func=AF.Copy),
                         reads=[("ps", b)], writes=gkey, phase=PFA)
                else:
                    S.op("dve", lambda e, src_=src_, dst_=dst_: e.tensor_copy(out=dst_, in_=src_),
                         reads=[("ps", b)], writes=gkey, phase=PFA)
                ev_ctr[0] += 1

        def chanfinal(ci):
            g, hh = combos_gh[ci]
            Gt = Gts[ci % NG]
            gkey = gkeys[ci % NG]

            def channel(k2l):
                b = newbank()
                fns = []
                for kc in range(KC):
                    fns.append(lambda e, b=b, kc=kc, k2l=k2l: e.matmul(
                        ps[b][:, kc * 256:(kc + 1) * 256], lhsT=Gt[:, k2l, kc * 128:(kc + 1) * 128],
                        rhs=csw[:, g, 128:384], start=True, stop=False))
                    fns.append(lambda e, b=b, kc=kc, k2l=k2l: e.matmul(
                        ps[b][:, kc * 256:(kc + 1) * 256], lhsT=Gt[:, 8 + k2l, kc * 128:(kc + 1) * 128],
                        rhs=csw[:, g, 0:256], start=False, stop=True))
                S.pe(fns, reads=gkey + [("csw", g)], writes=[("ps", b)], phase=PFA)
                r = h_ctr[0] % NR
                h_ctr[0] += 1
                k2 = 8 * hh + k2l
                dsto = Hq[:, r, :, :, :].rearrange("p k i d -> p (k i d)")
                srco = ps[b][:, 0:KC * 256]
                if h_ctr[0] % 2 == 0:
                    S.op("act", lambda e, dsto=dsto, srco=srco: e.activation(out=dsto, in_=srco, func=AF.Copy),
                         reads=[("ps", b)], writes=[("H", r)], phase=PFA)
                else:
                    S.op("dve", lambda e, dsto=dsto, srco=srco: e.tensor_copy(out=dsto, in_=srco),
                         reads=[("ps", b)], writes=[("H", r)], phase=PFA)
                return r

            def final(k2l, r, bF):
                fns = []
                n = 0
                for kc in range(KC):
                    for i in range(2):
                        fns.append(lambda e, kc=kc, i=i, r=r, k2l=k2l, bF=bF, n=n: e.matmul(
                            ps[bF][:, (k2l % 4) * 128:(k2l % 4 + 1) * 128], lhsT=Hq[:, r, kc, i, :],
                            rhs=Ekr[:, 8 * hh + k2l, kc, i, :], start=(n == 0), stop=(n == 2 * KC - 1)))
                        n += 1
                S.pe(fns, reads=[("H", r)] + ekkeys(r), writes=[("ps", bF)], phase=PFA)
                if k2l % 4 == 3:
                    k2base = 8 * hh + k2l - 3
                    dst_ = yaT(g).rearrange("p (n k) -> p k n", k=16)[:, k2base:k2base + 4, :]
                    src_ = ps[bF][:].rearrange("p (k n) -> p k n", k=4)
                    S.op("act", lambda e, dst_=dst_, src_=src_: e.activation(out=dst_, in_=src_, func=AF.Identity,
                                                                             bias=bfm_sb[:, g:g + 1]),
                         reads=[("ps", bF), "tabs"], writes=[("Y", g)], phase=[("AYU", PF)])

            return channel, final

        LAG = 3
        pend = []
        bFs = {}

        fin_ctr = [0]

        def do_final(item):
            fin, pk, pr, ci = item
            if pk % 4 == 0:
                bFs[ci] = 6 + fin_ctr[0] % 2
                fin_ctr[0] += 1
            fin(pk, pr, bFs[ci])

        bank_ring[:] = [0, 1, 2, 3, 4, 5]
        s2stage(0)
        s2stage(1)
        for ci in range(8):
            channel, final = chanfinal(ci)
            for k2l in range(8):
                r = channel(k2l)
                pend.append((final, k2l, r, ci))
                if len(pend) > LAG:
                    do_final(pend.pop(0))
            if ci + 2 < 8:
                s2stage(ci + 2)
        while pend:
            do_final(pend.pop(0))
        bank_ring[:] = list(range(8))

        if stage == 3:
            return
        sga = AR[:, 0:2048].rearrange("p (f n) -> p f n", f=4)
        sgb = AR[:, 2048:4096].rearrange("p (f n) -> p f n", f=4)
        tyb = AR[:, 4096:6144].rearrange("p (f n) -> p f n", f=4)
        vhat = AR[:, 6144:8192].rearrange("p (j c) -> p j c", j=4)
        sma = AY[:, 8192:12288].rearrange("p (k n) -> p k n", k=8)
        smb = AY[:, 12288:16384].rearrange("p (k n) -> p k n", k=8)
        phB = [("AR", PB)]
        phU = [("AYU", PB)]
        HK = [("hT0", kc) for kc in range(8)]

        pending = []

        def out_point(j):
            if pending:
                if j + 1 < 4:
                    pending[j + 1][0]()
                pending[j][1]()

        for t in range(4):
            for j in range(4):
                b = newbank()
                S.pe([lambda e, b=b, kc=kc, j=j: e.matmul(ps[b][:], lhsT=hT0[:, kc, j * 128:(j + 1) * 128],
                                                          rhs=w_in_sb[:, kc, V_OFF:V_OFF + 512],
                                                          start=(kc == 0), stop=(kc == 7)) for kc in range(8)],
                     reads=HK + ["w_in"], writes=[("ps", b)])
                S.op("dve", lambda e, b=b, j=j: e.bn_stats(out=bnst[:, j, :], in_=ps[b][:]),
                     reads=[("ps", b)], writes=[("bnst", j)])
                S.op("dve", lambda e, j=j: e.bn_aggr(out=bnag[:, j, 0:2], in_=bnst[:, j, :]),
                     reads=[("bnst", j)], writes=[("bnag", j)])
                S.op("act", lambda e, j=j: e.activation(out=bnag[:, j, 2:3], in_=bnag[:, j, 1:2], func=AF.Sqrt,
                                                         scale=1.0, bias=eps_sb[:, 0:1]),
                     reads=[("bnag", j), "eps"], writes=[("bnagS", j)])
                S.op("dve", lambda e, j=j: e.reciprocal(out=bnag[:, j, 2:3], in_=bnag[:, j, 2:3]),
                     reads=[("bnagS", j)], writes=[("bnag2", j)])
                S.op("dve", lambda e, j=j: e.scalar_tensor_tensor(out=bnag[:, j, 3:4], in0=bnag[:, j, 0:1], scalar=-1.0,
                                                                   in1=bnag[:, j, 2:3], op0=ALU.mult, op1=ALU.mult),
                     reads=[("bnag", j), ("bnag2", j)], writes=[("bnag3", j)])
                S.op("act", lambda e, b=b, j=j: e.activation(out=vhat[:, j, :], in_=ps[b][:], func=AF.Identity,
                                                             scale=bnag[:, j, 2:3], bias=bnag[:, j, 3:4]),
                     reads=[("ps", b), ("bnag2", j), ("bnag3", j)], writes=[("vhat", j)], phase=phB)

            def fm_group(off, fc):
                b = newbank()
                S.pe([lambda e, b=b, kc=kc: e.matmul(ps[b][:], lhsT=w_in_sb[:, kc, off + fc * 128: off + (fc + 1) * 128],
                                                     rhs=hT0[:, kc, :], start=(kc == 0), stop=(kc == 7)) for kc in range(8)],
                     reads=HK + ["w_in"], writes=[("ps", b)])
                return b

            out_point(0)
            for fc in range(4):
                b = fm_group(GB_OFF, fc)
                S.op("act", lambda e, b=b, fc=fc: e.activation(out=sgb[:, fc, :], in_=ps[b][:], func=AF.Silu),
                     reads=[("ps", b)], writes=[("sgb", fc)], phase=phB)
            out_point(1)
            for fc in range(4):
                b = fm_group(GA_OFF, fc)
                S.op("act", lambda e, b=b, fc=fc: e.activation(out=sga[:, fc, :], in_=ps[b][:], func=AF.Silu),
                     reads=[("ps", b)], writes=[("sga", fc)], phase=phB)
            out_point(2)
            for fc in range(4):
                b = fm_group(U_OFF, fc)
                S.op("dve", lambda e, b=b, fc=fc: e.tensor_tensor(out=sgb[:, fc, :], in0=ps[b][:], in1=sgb[:, fc, :], op=ALU.mult),
                     reads=[("ps", b), ("sgb", fc)], writes=[("sgb", fc)], phase=phB)
            if t + 1 < 4:
                fe1(rowsB(t + 1), FE_RING_B)
            elif u + 1 < 3 and stage == 0:
                fe1(rowsA_of(u + 1, 0), FE_RING_B)
            for fc in range(8):
                b = fm_group(MB_OFF, fc)
                S.op("act", lambda e, b=b, fc=fc: e.activation(out=smb[:, fc, :], in_=ps[b][:], func=AF.Sigmoid),
                     reads=[("ps", b)], writes=[("smb", fc)], phase=phU)
            out_point(3)
            for fc in range(8):
                b = fm_group(MA_OFF, fc)
                S.op("act", lambda e, b=b, fc=fc: e.activation(out=sma[:, fc, :], in_=ps[b][:], func=AF.Sigmoid),
                     reads=[("ps", b)], writes=[("sma", fc)], phase=phU)

            for g in range(4):
                S.op("pool", lambda e, g=g, t=t: e.tensor_tensor(out=sga[:, g, :], in0=yaT(g)[:, t * 512:(t + 1) * 512],
                                                                 in1=sga[:, g, :], op=ALU.mult),
                     reads=[("Y", g), ("sga", g)], writes=[("sga", g)], phase=phB + phU)

            if t + 1 < 4:
                fe2(hT0, "hT0", u)
            elif u + 1 < 3 and stage == 0:
                fe2(hT0, "hT0", u + 1)
                prefetched[u + 1] = True

            for h in range(4):
                b = newbank()
                S.pe([lambda e, b=b, h=h, j=j: e.matmul(ps[b][:, j * 128:(j + 1) * 128], lhsT=vhat[:, j, h * 128:(h + 1) * 128],
                                                        rhs=wsT[:, h, :], start=True, stop=True) for j in range(4)],
                     reads=[("vhat", j) for j in range(4)] + ["wsT"], writes=[("ps", b)], phase=phB)
                S.op("dve", lambda e, b=b, h=h: e.scalar_tensor_tensor(
                    out=tyb[:, h, :].rearrange("p (j q) -> p j q", j=4), in0=ps[b][:].rearrange("p (j q) -> p j q", j=4),
                    scalar=lng_sb[:, h:h + 1], in1=Ktab[:, h, :].unsqueeze(1).to_broadcast([128, 4, 128]),
                    op0=ALU.mult, op1=ALU.add),
                    reads=[("ps", b), "Ktab", "tabs"], writes=[("tyb", h)], phase=phB)
                S.op("pool", lambda e, h=h: e.tensor_tensor(out=tyb[:, h, :], in0=tyb[:, h, :], in1=sgb[:, h, :], op=ALU.mult),
                     reads=[("tyb", h), ("sgb", h)], writes=[("tyb", h)], phase=phB)

            for dmc in range(8):
                b = newbank()
                S.pe([lambda e, b=b, k=k, dmc=dmc: e.matmul(ps[b][:], lhsT=w_pa_sb[:, k, dmc * 128:(dmc + 1) * 128],
                                                            rhs=sga[:, k, :], start=(k == 0), stop=(k == 3)) for k in range(4)],
                     reads=[("sga", k) for k in range(4)] + ["w_pab"], writes=[("ps", b)], phase=phB)
                S.op("dve", lambda e, b=b, dmc=dmc: e.tensor_tensor(out=sma[:, dmc, :], in0=ps[b][:], in1=sma[:, dmc, :], op=ALU.mult),
                     reads=[("ps", b), ("sma", dmc)], writes=[("sma", dmc)], phase=phU)
            for dmc in range(8):
                b = newbank()
                S.pe([lambda e, b=b, k=k, dmc=dmc: e.matmul(ps[b][:], lhsT=w_pb_sb[:, k, dmc * 128:(dmc + 1) * 128],
                                                            rhs=tyb[:, k, :], start=(k == 0), stop=(k == 3)) for k in range(4)],
                     reads=[("tyb", k) for k in range(4)] + ["w_pab"], writes=[("ps", b)], phase=phB)
                S.op("dve", lambda e, b=b, dmc=dmc: e.tensor_tensor(out=smb[:, dmc, :], in0=ps[b][:], in1=smb[:, dmc, :], op=ALU.mult),
                     reads=[("ps", b), ("smb", dmc)], writes=[("smb", dmc)], phase=phU)
                S.op("pool", lambda e, dmc=dmc: e.tensor_tensor(out=sma[:, dmc, :], in0=sma[:, dmc, :], in1=smb[:, dmc, :], op=ALU.add),
                     reads=[("sma", dmc), ("smb", dmc)], writes=[("sma", dmc)], phase=phU)

            def make_chunk(t, j):
                st = {}
                row0 = u * UT + t * 512 + j * 128

                def load():
                    k = BE_RING[be_ctr[0] % len(BE_RING)]
                    be_ctr[0] += 1
                    st["k"] = k
                    S.dma("sp", [lambda e, k=k: e.dma_start(out=xy[:, k, :], in_=xu.ap()[row0:row0 + 128, :])],
                          "s_ld%d" % k, writes=[("xy", k)])

                def chunk(junk=junk):
                    k = st["k"]
                    si = st_ctr[0] % 8
                    st_ctr[0] += 1
                    for half in range(2):
                        b = newbank()
                        S.pe([lambda e, b=b, dmc=dmc, half=half: e.matmul(
                            ps[b][:], lhsT=sma[:, dmc, j * 128:(j + 1) * 128], rhs=w_out_sb[:, dmc, half * 512:(half + 1) * 512],
                            start=(dmc == 0), stop=(dmc == 7)) for dmc in range(8)],
                            reads=[("sma", dmc) for dmc in range(8)] + ["w_out"], writes=[("ps", b)], phase=phU)
                        S.op("dve", lambda e, b=b, k=k, half=half: e.tensor_tensor(
                            out=xy[:, k, half * 512:(half + 1) * 512], in0=ps[b][:], in1=xy[:, k, half * 512:(half + 1) * 512],
                            op=ALU.add),
                            reads=[("ps", b), ("xy", k)], writes=[("xy", k)])
                    S.op("act", lambda e, k=k, si=si: e.activation(out=junk, in_=xy[:, k, :], func=AF.Square,
                                                                   accum_out=stat[:, si, 0:1]),
                         reads=[("xy", k)], writes=[("st", si), "junk"], phase=phU)
                    S.op("act", lambda e, si=si: e.activation(out=stat[:, si, 1:2], in_=stat[:, si, 0:1], func=AF.Sqrt,
                                                               scale=1.0 / D, bias=eps_sb[:, 0:1]),
                         reads=[("st", si), "eps"], writes=[("st1", si)])
                    S.op("dve", lambda e, si=si: e.reciprocal(out=stat[:, si, 2:3], in_=stat[:, si, 1:2]),
                         reads=[("st1", si)], writes=[("st2", si)])
                    S.op("dve", lambda e, k=k, si=si: e.scalar_tensor_tensor(out=xy[:, k, :], in0=xy[:, k, :],
                                                                              scalar=stat[:, si, 2:3], in1=fg_bc[:],
                                                                              op0=ALU.mult, op1=ALU.mult),
                         reads=[("xy", k), ("st2", si), "tabs"], writes=[("xy", k)])
                    tok = S.dma("sp", [lambda e, k=k: e.dma_start(out=yu.ap()[row0:row0 + 128, :], in_=xy[:, k, :])],
                                "s_st%d" % k, reads=[("xy", k)], writes=[("yout", row0)])
                    out_tokens.append(tok)
                return load, chunk

            pending[:] = [make_chunk(t, j) for j in range(4)]
            pending[0][0]()

        if u + 1 < 3 and stage == 0:
            pending_x[:] = list(pending)
        else:
            for j in range(4):
                if j + 1 < 4:
                    pending[j + 1][0]()
                pending[j][1]()
        pending[:] = []

    for u in range(3):
        do_unit(u)
        if stage in (2, 3, 4, 21, 22, 23, 31, 32, 33, 34, 35):
            break

    last = {}
    for sk, v in out_tokens:
        last[sk] = max(last.get(sk, 0), v)
    for sk, v in last.items():
        S.streams["sp"].append(("wait", sk, v))

    with nc.Block() as block:
        def run(stream, e):
            for it in stream:
                if it[0] == "wait":
                    e.wait_ge(sems[it[1]], it[2])
                elif it[0] == "raw":
                    it[1](e)
                else:
                    ins = it[1](e)
                    ins.then_inc(sems[it[2]], it[3])

        @block.tensor
        def _(e):
            run(S.streams["pe"], e)

        @block.scalar
        def _(e):
            run(S.streams["act"], e)

        @block.vector
        def _(e):
            run(S.streams["dve"], e)

        @block.gpsimd
        def _(e):
            run(S.streams["pool"], e)

        @block.sync
        def _(e):
            run(S.streams["sp"], e)

    es.close()
    return nc


_NC_CACHE = {}
_STAGE = 0


def kernel(x_prompt, x_sample, c_prompt, c_sample, norm_g, w_ada, b_ada, w_in, w_fmix, b_fmix,
           sgu_ln_g, sgu_ln_b, w_s, b_s, w_pa, w_pb, w_out, final_g):
    f = lambda a: np.ascontiguousarray(np.asarray(a, dtype=np.float32))
    x_prompt, x_sample, c_prompt, c_sample = f(x_prompt), f(x_sample), f(c_prompt), f(c_sample)
    norm_g, w_ada, b_ada, w_in = f(norm_g)[0], f(w_ada)[0], f(b_ada)[0], f(w_in)[0]
    w_fmix, b_fmix = f(w_fmix)[0], f(b_fmix)[0]
    sgu_ln_g, sgu_ln_b, w_s, b_s = f(sgu_ln_g)[0], f(sgu_ln_b)[0], f(w_s)[0], f(b_s)[0]
    w_pa, w_pb, w_out, final_g = f(w_pa)[0], f(w_pb)[0], f(w_out)[0], f(final_g)

    if "nc" not in _NC_CACHE:
        _NC_CACHE["nc"] = build_nc(_STAGE)
    nc = _NC_CACHE["nc"]

    shared = {
        "w_ada": w_ada, "w_in": w_in, "w_pa": w_pa, "w_pb": w_pb, "w_out": w_out,
        "b_adaT": f(b_ada.reshape(24, 128).T), "norm_gT": f(norm_g.reshape(8, 128).T),
        "wfm": f(w_fmix.transpose(1, 0, 2).reshape(128, 512)),
        "b_fmixT": f(b_fmix.T), "ln_gT": f(sgu_ln_g.reshape(4, 128).T), "ln_b": sgu_ln_b,
        "w_s": f(w_s.reshape(512, 128)), "b_s": f(b_s.reshape(512)), "final_g": final_g,
    }
    in_maps = []
    for k in range(NCORES):
        q, hf = k // 2, k % 2
        cst = _consts(hf)
        c3 = np.stack([c_sample[q], c_prompt[2 * k], c_prompt[2 * k + 1]], axis=0)
        cTk = f(c3.reshape(3, 8, 128).transpose(2, 1, 0).reshape(128, 24))
        xuk = np.concatenate([x_sample[q, hf * UT:(hf + 1) * UT], x_prompt[2 * k], x_prompt[2 * k + 1]], axis=0)
        m = dict(shared)
        m.update({"xs_full": x_sample[q], "xu": f(xuk), "cT": cTk})
        m.update(cst)
        in_maps.append(m)

    res = run_bass_kernel_spmd(nc, in_maps, core_ids=list(range(NCORES)))
    y_prompt = np.empty_like(x_prompt)
    y_sample = np.empty_like(x_sample)
    for k in range(NCORES):
        q, hf = k // 2, k % 2
        yu = np.asarray(res.results[k]["yu"], dtype=np.float32)
        y_sample[q, hf * UT:(hf + 1) * UT] = yu[0:UT]
        y_prompt[2 * k] = yu[UT:2 * UT]
        y_prompt[2 * k + 1] = yu[2 * UT:3 * UT]
    return (y_prompt, y_sample)
```

```python
import numpy as np
import ml_dtypes
import concourse.bass as bass
import concourse.mybir as mybir
from concourse.bass_utils import run_bass_kernel_spmd

F32 = mybir.dt.float32
BF16 = mybir.dt.bfloat16
AF = mybir.ActivationFunctionType
ALU = mybir.AluOpType

D = 1024
INW = 4608
EPS = 1e-6
NCORES = 8
UT = 2048
A_OFF, GA_OFF, U_OFF, V_OFF, GB_OFF, MA_OFF, MB_OFF = 0, 512, 1024, 1536, 2048, 2560, 3584


class Sched:
    ENGS = ("pe", "act", "dve", "pool", "sp")

    def __init__(self):
        self.streams = {e: [] for e in self.ENGS}
        self.count = {}
        self.waited = {e: {} for e in self.ENGS}
        self.last_w = {}
        self.readers = {}
        self.phase_users = {}

    def _waits(self, eng, reads, writes, phase, extra):
        deps = {}

        def add(tok, raw):
            sk, v = tok
            if sk == eng and (eng == "pe" or not raw):
                return
            if deps.get(sk, 0) < v:
                deps[sk] = v

        for b in reads:
            t = self.last_w.get(b)
            if t:
                add(t, True)
        for b in writes:
            t = self.last_w.get(b)
            if t:
                add(t, False)
            for r in self.readers.get(b, ()):
                add(r, False)
        for t in extra:
            add(t, True)
        for (res, ph) in phase:
            prev = self.phase_users.get((res, ph - 1))
            if prev:
                for sk, v in prev.items():
                    add((sk, v), False)
        for sk, v in deps.items():
            if self.waited[eng].get(sk, 0) < v:
                self.streams[eng].append(("wait", sk, v))
                self.waited[eng][sk] = v

    def _update(self, reads, writes, phase, tok):
        for b in reads:
            self.readers.setdefault(b, []).append(tok)
        for b in writes:
            self.last_w[b] = tok
            self.readers[b] = []
        for key in phase:
            d = self.phase_users.setdefault(key, {})
            if d.get(tok[0], 0) < tok[1]:
                d[tok[0]] = tok[1]

    def op(self, eng, fn, reads=(), writes=(), phase=(), extra=()):
        self._waits(eng, reads, writes, phase, extra)
        self.count[eng] = self.count.get(eng, 0) + 1
        tok = (eng, self.count[eng])
        self.streams[eng].append(("op", fn, eng, 1))
        self._update(reads, writes, phase, tok)
        return tok

    def pe(self, fns, reads=(), writes=(), phase=(), extra=()):
        self._waits("pe", reads, writes, phase, extra)
        for f in fns[:-1]:
            self.streams["pe"].append(("raw", f))
        self.count["pe"] = self.count.get("pe", 0) + 1
        tok = ("pe", self.count["pe"])
        self.streams["pe"].append(("op", fns[-1], "pe", 1))
        self._update(reads, writes, phase, tok)
        return tok

    def dma(self, queue, fns, sem, reads=(), writes=(), phase=(), extra=()):
        self._waits(queue, reads, writes, phase, extra)
        for f in fns:
            self.count[sem] = self.count.get(sem, 0) + 16
            self.streams[queue].append(("op", f, sem, 16))
        tok = (sem, self.count[sem])
        self._update(reads, writes, phase, tok)
        return tok


def _consts(hf):
    c = {}
    c["identb"] = np.eye(128, dtype=np.float32).astype(ml_dtypes.bfloat16)
    c["identf"] = np.eye(128, dtype=np.float32)
    c["onesf"] = np.ones((128, 128), dtype=np.float32)
    cc = np.arange(128)
    ph = 2.0 * np.pi * np.outer(cc, cc) / 128.0
    c["cosC"] = np.cos(ph).astype(np.float32)
    c["sinC"] = np.sin(ph).astype(np.float32)
    r2 = np.zeros((128, 256), dtype=np.float64)
    for s1j in range(8):
        for s2 in range(16):
            p = 8 * s2 + s1j
            for hh in range(2):
                for k2l in range(8):
                    k2 = 8 * hh + k2l
                    ang = 2.0 * np.pi * s2 * k2 / 16.0
                    r2[p, hh * 128 + 0 * 64 + k2l * 8 + s1j] = np.cos(ang)
                    r2[p, hh * 128 + 1 * 64 + k2l * 8 + s1j] = -np.sin(ang)
    c["r2h"] = r2.astype(np.float32)
    for u, (S, off) in enumerate(((4096, 128 * hf), (2048, 0))):
        N1 = S // 16
        KC = N1 // 128
        scale = 1.0 / np.sqrt(S * 128.0)
        s1 = (128 * np.arange(KC)[None, :] + np.arange(128)[:, None]).astype(np.float64)
        k = 16.0 * (np.arange(128) + off)[None, :] + np.arange(16)[:, None]
        th = 2.0 * np.pi * s1[None, :, :, None] * k[:, None, None, :] / S
        E = np.stack([np.cos(th), np.sin(th)], axis=3) * scale
        c["Ek%d" % u] = E.reshape(16 * 128, KC * 2 * 128).astype(np.float32)
    return c


_CONST_SHAPES = None


def build_nc(stage=0):
    nc = bass.Bass("TRN2", target_bir_lowering=False)
    S = Sched()

    def din(name, shape, dt=F32):
        return nc.dram_tensor(name, list(shape), dt, kind="ExternalInput")

    xs_full = din("xs_full", [4096, D])
    xu = din("xu", [3 * UT, D])
    cT = din("cT", [128, 24])
    w_ada = din("w_ada", [D, 3 * D])
    b_adaT = din("b_adaT", [128, 24])
    norm_gT = din("norm_gT", [128, 8])
    w_in = din("w_in", [D, INW])
    w_pa = din("w_pa", [512, D])
    w_pb = din("w_pb", [512, D])
    w_out = din("w_out", [D, D])
    wfm = din("wfm", [128, 512])
    b_fmixT = din("b_fmixT", [128, 4])
    ln_gT = din("ln_gT", [128, 4])
    ln_b = din("ln_b", [512])
    w_s = din("w_s", [512, 128])
    b_s = din("b_s", [512])
    final_g = din("final_g", [D])
    identb = din("identb", [128, 128], BF16)
    identf = din("identf", [128, 128])
    onesf = din("onesf", [128, 128])
    cosC = din("cosC", [128, 128])
    sinC = din("sinC", [128, 128])
    r2h = din("r2h", [128, 256])
    Ek0 = din("Ek0", [16 * 128, 2 * 2 * 128])
    Ek1 = din("Ek1", [16 * 128, 1 * 2 * 128])
    yu = nc.dram_tensor("yu", [3 * UT, D], F32, kind="ExternalOutput")

    from contextlib import ExitStack
    es = ExitStack()

    def sb(name, shape, dt):
        return es.enter_context(nc.sbuf_tensor(name, list(shape), dt))

    w_in_sb = sb("w_in_sb", [128, 8, INW], BF16)
    w_pa_sb = sb("w_pa_sb", [128, 4, D], BF16)
    w_pb_sb = sb("w_pb_sb", [128, 4, D], BF16)
    w_out_sb = sb("w_out_sb", [128, 8, D], BF16)
    AY = sb("AY", [128, 16384], BF16)
    Y0 = sb("Y0", [128, 2048], BF16)
    AR = sb("AR", [128, 8192], BF16)
    hT0 = sb("hT0", [128, 8, 512], BF16)
    xsr = sb("xsr", [128, 4, D], BF16)
    NXY = 4
    xy = sb("xy", [128, NXY, D], F32)
    gate_bc = sb("gate_bc", [128, D], F32)
    fg_bc = sb("fg_bc", [128, D], F32)
    csw = sb("csw", [128, 4, 384], BF16)
    identb_sb = sb("identb_sb", [128, 128], BF16)
    identf_sb = sb("identf_sb", [128, 128], F32)
    onesf_sb = sb("onesf_sb", [128, 128], F32)
    dgt = sb("dgt", [128, 2, 128], F32)
    r2_sb = sb("r2_sb", [128, 256], BF16)
    wsT = sb("wsT", [128, 4, 128], BF16)
    Ktab = sb("Ktab", [128, 4, 128], BF16)
    cT_sb = sb("cT_sb", [128, 24], F32)
    cs_sb = sb("cs_sb", [128, 24], F32)
    badaT_sb = sb("badaT_sb", [128, 24], F32)
    ngT_sb = sb("ngT_sb", [128, 8], F32)
    modT = sb("modT", [128, 24, 3], F32)
    Gtab = sb("Gtab", [128, 8, 3], F32)
    Gtmp = sb("Gtmp", [128, 8, 3], F32)
    bfm_sb = sb("bfm_sb", [128, 4], F32)
    lng_sb = sb("lng_sb", [128, 4], F32)
    ARf = AR[:, 0:3584].bitcast(F32)
    cosC_sb = ARf[:, 0:128]
    sinC_sb = ARf[:, 128:256]
    wfm_sb = ARf[:, 256:768]
    lnb_f = ARf[:, 768:1280]
    bs_bc = ARf[:, 1280:1792]
    lnb_b = AR[:, 3584:4096]
    ws_nat = AR[:, 4096:4608].rearrange("p (h q) -> p h q", h=4)
    junk = AY[:, 6144:7168]
    eps_sb = sb("eps_sb", [128, 1], F32)
    stat = sb("stat", [128, 8, 4], F32)
    bnst = sb("bnst", [128, 4, 6], F32)
    bnag = sb("bnag", [128, 4, 4], F32)
    ps = [es.enter_context(nc.psum_tensor("ps%d" % i, [128, 512], F32)) for i in range(8)]

    semnames = ["pe", "act", "dve", "pool", "s_win", "s_wina", "s_wrest", "s_wpab", "s_tab", "s_wada0", "s_wada1",
                "s_ld0", "s_ld1", "s_ld2", "s_ld3", "s_ld4", "s_ld5", "s_ld6", "s_ek0", "s_ek1", "s_ek2", "s_ek3", "s_ek4", "s_ek5", "s_ek6", "s_ek7", "s_st0", "s_st1", "s_st2", "s_st3", "s_wout"]
    sems = {n: es.enter_context(nc.semaphore(n)) for n in semnames}

    bank_ctr = [0]
    bank_ring = list(range(8))

    def newbank():
        b = bank_ring[bank_ctr[0] % len(bank_ring)]
        bank_ctr[0] += 1
        return b

    def psb(b):
        return ps[b][:].bitcast(BF16)

    def dram_ap(t, offset, dims):
        return bass.AP(t.ap().tensor, offset, [list(d) for d in dims])

    tabs = []

    def tab(dst_ap, src_ap):
        tabs.append(lambda e, d=dst_ap, s=src_ap: e.dma_start(out=d, in_=s))

    tab(cT_sb[:], cT.ap())
    tab(badaT_sb[:], b_adaT.ap())
    tab(ngT_sb[:], norm_gT.ap())
    tab(identf_sb[:], identf.ap())
    tab(onesf_sb[:], onesf.ap())
    tab(identb_sb[:], identb.ap())
    tab(cosC_sb, cosC.ap())
    tab(sinC_sb, sinC.ap())
    tab(wfm_sb, wfm.ap())
    tab(bfm_sb[:], b_fmixT.ap())
    tab(lng_sb[:], ln_gT.ap())
    tab(lnb_f, dram_ap(ln_b, 0, [[0, 128], [1, 512]]))
    tab(bs_bc, dram_ap(b_s, 0, [[0, 128], [1, 512]]))
    tab(fg_bc[:], dram_ap(final_g, 0, [[0, 128], [1, D]]))
    TABKEYS = ["tabs"]
    S.dma("sp", tabs, "s_tab", writes=TABKEYS, phase=[("AR", 0)])

    AYf = AY[:].bitcast(F32)
    wada_blk = [AYf[:, 0:4096].rearrange("p (k n) -> p k n", k=8),
                AYf[:, 4096:8192].rearrange("p (k n) -> p k n", k=8)]

    def wada_dma(cb):
        src = dram_ap(w_ada, cb * 512, [[3 * D, 128], [128 * 3 * D, 8], [1, 512]])
        return lambda e, s=src, d=wada_blk[cb % 2]: e.dma_start(out=d, in_=s)

    wla, wl = [], []
    for kc in range(8):
        wla.append(lambda e, kc=kc: e.dma_start(out=w_in_sb[:, kc, 0:512], in_=w_in.ap()[kc * 128:(kc + 1) * 128, 0:512]))
    for cb in range(2):
        for kc in range(8):
            wl.append(lambda e, kc=kc, cb=cb: e.dma_start(
                out=w_in_sb[:, kc, 512 + cb * 2048: 512 + (cb + 1) * 2048],
                in_=w_in.ap()[kc * 128:(kc + 1) * 128, 512 + cb * 2048: 512 + (cb + 1) * 2048]))
    wr = []
    wpab = []
    for k in range(4):
        wpab.append(lambda e, k=k: e.dma_start(out=w_pa_sb[:, k, :], in_=w_pa.ap()[k * 128:(k + 1) * 128, :]))
        wpab.append(lambda e, k=k: e.dma_start(out=w_pb_sb[:, k, :], in_=w_pb.ap()[k * 128:(k + 1) * 128, :]))
    wr.append(lambda e: e.dma_start(out=r2_sb[:], in_=r2h.ap()))
    wr.append(lambda e: e.dma_start(out=ws_nat, in_=dram_ap(w_s, 0, [[128, 128], [128 * 128, 4], [1, 128]])))

    S.op("dve", lambda e: e.memset(eps_sb[:], EPS), writes=["eps"])
    S.op("act", lambda e: e.activation(out=cs_sb[:], in_=cT_sb[:], func=AF.Silu),
         reads=TABKEYS, writes=["cs"])
    bmod = newbank()
    for cb in range(6):
        S.dma("act", [wada_dma(cb)], "s_wada%d" % (cb % 2), writes=[("wada", cb % 2)], phase=[("AYU", 0)])
        fns = []
        for ch in range(4):
            chunk = cb * 4 + ch
            for kc in range(8):
                fns.append(lambda e, chunk=chunk, ch=ch, kc=kc, cb=cb: e.matmul(
                    ps[bmod][:, chunk * 3:chunk * 3 + 3],
                    lhsT=wada_blk[cb % 2][:, kc, ch * 128:(ch + 1) * 128],
                    rhs=cs_sb[:, kc * 3:kc * 3 + 3],
                    start=(kc == 0), stop=(kc == 7)))
        S.pe(fns, reads=[("wada", cb % 2), "cs"], writes=[("ps", bmod)], phase=[("AYU", 0)])
    S.dma("pool", wla, "s_wina", writes=["w_in_a"])
    S.dma("pool", wr, "s_wrest", writes=["w_rest"], phase=[("AR", 0)])

    S.op("dve", lambda e: e.tensor_tensor(out=modT[:], in0=ps[bmod][:, 0:72].rearrange("p (c b) -> p c b", b=3),
                                          in1=badaT_sb[:].unsqueeze(2).to_broadcast([128, 24, 3]), op=ALU.add),
         reads=[("ps", bmod), "tabs"], writes=["modT"])
    S.op("dve", lambda e: e.tensor_scalar(out=Gtmp[:], in0=modT[:, 8:16, :], scalar1=1.0, scalar2=None, op0=ALU.add),
         reads=["modT"], writes=["Gtmp"])
    S.op("dve", lambda e: e.tensor_tensor(out=Gtab[:], in0=Gtmp[:],
                                          in1=ngT_sb[:].unsqueeze(2).to_broadcast([128, 8, 3]), op=ALU.mult),
         reads=["Gtmp", "tabs"], writes=["Gtab"])

    P0 = [("AR", 0)]
    for g in range(4):
        b = newbank()
        S.pe([lambda e, g=g, b=b: e.matmul(ps[b][:, 0:128], lhsT=cosC_sb, rhs=wfm_sb[:, g * 128:(g + 1) * 128],
                                           start=True, stop=True),
              lambda e, g=g, b=b: e.matmul(ps[b][:, 128:256], lhsT=sinC_sb, rhs=wfm_sb[:, g * 128:(g + 1) * 128],
                                           start=True, stop=True)],
             reads=["tabs"], writes=[("ps", b)], phase=P0)
        S.op("act", lambda e, g=g, b=b: e.activation(out=csw[:, g, 128:256], in_=ps[b][:, 0:128], func=AF.Copy),
             reads=[("ps", b)], writes=[("csw", g)])
        S.op("act", lambda e, g=g, b=b: e.activation(out=csw[:, g, 256:384], in_=ps[b][:, 128:256], func=AF.Copy, scale=-1.0),
             reads=[("ps", b)], writes=[("csw", g)])
        S.op("act", lambda e, g=g, b=b: e.activation(out=csw[:, g, 0:128], in_=ps[b][:, 128:256], func=AF.Copy),
             reads=[("ps", b)], writes=[("csw", g)])

    bt = newbank()
    S.pe([lambda e, h=h: e.transpose(psb(bt)[:, h * 128:(h + 1) * 128], ws_nat[:, h, :], identb_sb[:]) for h in range(4)],
         reads=["w_rest", "tabs"], writes=[("ps", bt)], phase=P0)
    S.op("act", lambda e: e.activation(out=wsT[:].rearrange("p h q -> p (h q)"), in_=psb(bt)[:, 0:512], func=AF.Copy),
         reads=[("ps", bt)], writes=["wsT"])
    S.op("dve", lambda e: e.tensor_copy(out=lnb_b, in_=lnb_f), reads=["tabs"], writes=["lnb_b"], phase=P0)
    bk = newbank()
    S.pe([lambda e, h=h: e.matmul(ps[bk][:, h * 128:(h + 1) * 128], lhsT=lnb_b[:, h * 128:(h + 1) * 128], rhs=wsT[:, h, :],
                                  start=True, stop=True) for h in range(4)],
         reads=["lnb_b", "wsT"], writes=[("ps", bk)], phase=P0)
    S.op("dve", lambda e: e.tensor_tensor(out=Ktab[:].rearrange("p h q -> p (h q)"), in0=ps[bk][:, 0:512], in1=bs_bc, op=ALU.add),
         reads=[("ps", bk), "tabs"], writes=["Ktab"], phase=P0)

    fe_ctr = [0]
    be_ctr = [0]
    st_ctr = [0]
    FE_RING_A = [0, 1, 4, 5, 6]
    FE_RING_B = [0, 1]
    BE_RING = [2, 3]

    def xys(k):
        if k < NXY:
            return xy[:, k, :]
        if k == 6:
            return Y0[:].bitcast(F32)
        return AR[:, 4096 + 2048 * (k - 4): 4096 + 2048 * (k - 3)].bitcast(F32)

    def fe1(row_aps, ring, phase=()):
        for j, src in enumerate(row_aps):
            k = ring[fe_ctr[0] % len(ring)]
            fe_ctr[0] += 1
            si = st_ctr[0] % 8
            st_ctr[0] += 1
            ph = phase if k in (4, 5) else ()
            xk = [("xy", k)] + ([("Y", 0)] if k == 6 else [])
            S.dma("sp", [lambda e, k=k, src=src: e.dma_start(out=xys(k), in_=src)], "s_ld%d" % k,
                  writes=xk, phase=ph)
            S.op("act", lambda e, k=k, j=j, si=si: e.activation(out=xsr[:, j, :], in_=xys(k), func=AF.Square,
                                                                 accum_out=stat[:, si, 0:1]),
                 reads=xk, writes=[("xs", j), ("st", si)], phase=ph)
            S.op("act", lambda e, si=si: e.activation(out=stat[:, si, 1:2], in_=stat[:, si, 0:1], func=AF.Sqrt,
                                                       scale=1.0 / D, bias=eps_sb[:, 0:1]),
                 reads=[("st", si), "eps"], writes=[("st1", si)])
            S.op("dve", lambda e, si=si: e.reciprocal(out=stat[:, si, 2:3], in_=stat[:, si, 1:2]),
                 reads=[("st1", si)], writes=[("st2", si)])
            S.op("dve", lambda e, k=k, j=j, si=si: e.tensor_scalar(out=xsr[:, j, :], in0=xys(k),
                                                                   scalar1=stat[:, si, 2:3], scalar2=None, op0=ALU.mult),
                 reads=xk + [("st2", si)], writes=[("xs", j)], phase=ph)

    def fe2(hT, hkey, slot, phase=()):
        for kp in range(4):
            b = newbank()
            fns = []
            for kk in range(2):
                kc = 2 * kp + kk
                for j in range(4):
                    fns.append(lambda e, b=b, kk=kk, kc=kc, j=j: e.transpose(
                        psb(b)[:, kk * 512 + j * 128: kk * 512 + (j + 1) * 128],
                        xsr[:, j, kc * 128:(kc + 1) * 128], identb_sb[:]))
            S.pe(fns, reads=[("xs", j) for j in range(4)] + ["tabs"], writes=[("ps", b)])
            for kk in range(2):
                kc = 2 * kp + kk
                S.op("dve", lambda e, b=b, kk=kk, kc=kc: e.tensor_scalar(
                    out=hT[:, kc, :], in0=psb(b)[:, kk * 512:(kk + 1) * 512],
                    scalar1=Gtab[:, kc, slot:slot + 1], scalar2=modT[:, kc, slot:slot + 1],
                    op0=ALU.mult, op1=ALU.add),
                    reads=[("ps", b), "Gtab", "modT"], writes=[(hkey, kc)], phase=phase)

    hT1 = AR[:, 0:4096].rearrange("p (k n) -> p k n", k=8)
    hTs = [hT0, hT1]

    out_tokens = []

    def rowsA_of(uu, s):
        Sin_ = 4096 if uu == 0 else 2048
        N1_ = Sin_ // 16
        xsrc_ = xs_full if uu == 0 else xu
        xbase_ = 0 if uu == 0 else uu * UT
        return [dram_ap(xsrc_, (xbase_ + 8 * (4 * s + j)) * D, [[N1_ * D, 16], [D, 8], [1, D]]) for j in range(4)]

    prefetched = {}
    pending_x = []

    def do_unit(u):
        if stage == 1:
            return
        Sin = 4096 if u == 0 else 2048
        N1 = Sin // 16
        KC = N1 // 128
        NT = Sin // 128
        NS = NT // 4
        Ek_d = Ek0 if u == 0 else Ek1
        xsrc = xs_full if u == 0 else xu
        xbase = 0 if u == 0 else u * UT
        PA, PF, PB = 3 * u + 1, 3 * u + 2, 3 * u + 3
        FE_RING_A = [0, 1, 2, 3, 4, 5, 6] if u == 0 else [0, 1, 4, 5, 6]
        A_tm = AY[:, 0:4 * NT * 128].rearrange("p (g b c) -> p g b c", g=4, b=NT)

        def yaT(g):
            return Y0[:, :] if g == 0 else AY[:, (g - 1) * 2048: g * 2048]

        def gate_pre(kc):
            S.op("dve", lambda e, kc=kc: e.tensor_scalar(out=dgt[:, kc % 2, :], in0=identf_sb[:],
                                                        scalar1=modT[:, 16 + kc, u:u + 1], scalar2=None, op0=ALU.mult),
                 reads=["modT", "tabs"], writes=[("dgt", kc % 2)])

        def gate_post(kc):
            b = newbank()
            S.pe([lambda e, b=b, kc=kc: e.matmul(ps[b][:, 0:128], lhsT=onesf_sb[:], rhs=dgt[:, kc % 2, :], start=True, stop=True)],
                 reads=[("dgt", kc % 2), "tabs"], writes=[("ps", b)])
            S.op("act", lambda e, b=b, kc=kc: e.activation(out=gate_bc[:, kc * 128:(kc + 1) * 128], in_=ps[b][:, 0:128], func=AF.Copy),
                 reads=[("ps", b)], writes=["gate_bc"])

        def rowsA(s):
            aps = []
            for j in range(4):
                s1b = 4 * s + j
                aps.append(dram_ap(xsrc, (xbase + 8 * s1b) * D, [[N1 * D, 16], [D, 8], [1, D]]))
            return aps

        def passA_mm(s):
            hT = hTs[s % 2]
            hkey = "hT%d" % (s % 2)
            for j in range(4):
                s1b = 4 * s + j
                b = newbank()
                S.pe([lambda e, b=b, kc=kc, j=j, hT=hT: e.matmul(ps[b][:], lhsT=hT[:, kc, j * 128:(j + 1) * 128],
                                                                  rhs=w_in_sb[:, kc, A_OFF:A_OFF + 512],
                                                                  start=(kc == 0), stop=(kc == 7)) for kc in range(8)],
                     reads=[(hkey, kc) for kc in range(8)] + ["w_in_a"], writes=[("ps", b)],
                     phase=[("AR", PA)] if s % 2 == 1 else ())
                S.op("act", lambda e, b=b, s1b=s1b: e.activation(out=A_tm[:, :, s1b, :],
                                                                 in_=ps[b][:].rearrange("p (g c) -> p g c", g=4), func=AF.Copy),
                     reads=[("ps", b)], writes=[("A", g) for g in range(4)] + [("Y", g) for g in range(1, 4)],
                     phase=[("AYU", PA)])

        def rowsB(t):
            return [xu.ap()[u * UT + t * 512 + j * 128: u * UT + t * 512 + (j + 1) * 128, :] for j in range(4)]

        def fe2A(s):
            fe2(hTs[s % 2], "hT%d" % (s % 2), u, phase=[("AR", PA)] if s % 2 == 1 else ())

        if stage == 21:
            return
        if not prefetched.get(u):
            fe1(rowsA(0), FE_RING_A, [("AR", PA)])
        if stage == 22:
            return
        if not prefetched.get(u):
            fe2A(0)
        if stage == 23:
            return
        if NS > 1:
            fe1(rowsA(1), FE_RING_A, [("AR", PA)])
        gsteps = 8 // NS
        for s in range(NS):
            for q in range(gsteps):
                gate_pre(s * gsteps + q)
            if s + 1 < NS:
                fe2A(s + 1)
            else:
                fe2(hT0, "hT0", u)
            if s + 2 < NS:
                fe1(rowsA(s + 2), FE_RING_A, [("AR", PA)])
            elif s + 2 == NS:
                fe1(rowsB(0), FE_RING_A, [("AR", PA)])
            passA_mm(s)
            for q in range(gsteps):
                gate_post(s * gsteps + q)
            if pending_x and s < 4:
                if s + 1 < 4:
                    pending_x[s + 1][0]()
                pending_x[s][1](AY[:, 12288:13312])
                if s == 3:
                    pending_x[:] = []
        nslot = 4 if KC == 2 else 2
        xyb = xy[:, 0:nslot, :].rearrange("p s d -> p (s d)").bitcast(BF16)
        Ekr = xyb[:, 0:16 * KC * 256].rearrange("p (r k i d) -> p r k i d", r=16, k=KC, i=2)
        EKK = ["Ek"] + [("xy", s_) for s_ in range(nslot)]
        S.dma("pool", [lambda e, q=q: e.dma_start(
            out=Ekr[:, 4 * q:4 * q + 4, :, :, :].rearrange("p r k i d -> p r (k i d)"),
            in_=dram_ap(Ek_d, 4 * q * 128 * KC * 256, [[KC * 256, 128], [128 * KC * 256, 4], [1, KC * 256]]))
            for q in range(4)], "s_ek0", writes=EKK)

        S.dma("pool", [lambda e, k=k: e.dma_start(out=w_out_sb[:, k, :], in_=w_out.ap()[k * 128:(k + 1) * 128, :])
                       for k in range(8)], "s_wout", writes=["w_out"])
        for k in range(8):
            S.op("pool", lambda e, k=k: e.tensor_tensor(out=w_out_sb[:, k, :], in0=w_out_sb[:, k, :], in1=gate_bc[:], op=ALU.mult),
                 reads=["w_out", "gate_bc"], writes=["w_out"])
        if u == 0:
            S.dma("pool", wl, "s_win", writes=["w_in"])
            S.dma("pool", wpab, "s_wpab", writes=["w_pab"])

        if stage == 2:
            return
        if stage == 31:
            return

        combos_gh_pre = [(g, hh) for g in range(4) for hh in range(2)]
        NG = 2
        if KC == 1:
            Gts = [AR[:, i * 16 * N1:(i + 1) * 16 * N1].rearrange("p (q s) -> p q s", q=16) for i in range(NG)]
            gkeys = [[("G", 0)], [("G", 1)]]
        else:
            Gts = [AR[:, 0:16 * N1].rearrange("p (q s) -> p q s", q=16),
                   xsr[:].rearrange("p j d -> p (j d)").rearrange("p (q s) -> p q s", q=16)]
            gkeys = [[("G", 0)], [("xs", j) for j in range(4)]]
        NR = 8
        Hq = AR[:, 4096:4096 + NR * KC * 256].rearrange("p (r k i d) -> p r k i d", r=NR, k=KC, i=2)
        def ekkeys(r):
            return EKK

        ev_ctr = [0]
        h_ctr = [0]
        PFA = [("AR", PF)]
        combos_gh = [(g, hh) for g in range(4) for hh in range(2)]

        def s2stage(ci):
            g, hh = combos_gh[ci]
            Gt = Gts[ci % NG]
            gkey = gkeys[ci % NG]
            for sb4 in range(NT // 4):
                b = newbank()
                S.pe([lambda e, b=b, bb=bb, sb4=sb4: e.matmul(
                    ps[b][:, bb * 128:(bb + 1) * 128], lhsT=A_tm[:, g, 4 * sb4 + bb, :],
                    rhs=r2_sb[:, hh * 128:(hh + 1) * 128], start=True, stop=True) for bb in range(4)],
                    reads=[("A", g), "w_rest"], writes=[("ps", b)], phase=[("AYU", PF)])
                src_ = ps[b][:].rearrange("p (b q j) -> p b q j", b=4, q=16)
                dst_ = Gt[:, :, 32 * sb4: 32 * sb4 + 32].rearrange("p q (b j) -> p b q j", b=4)
                if ev_ctr[0] % 4 == 0:
                    S.op("act", lambda e, src_=src_, dst_=dst_: e.activation(out=dst_, in_=src_, func=AF.Copy),
                         reads=[("ps", b)], writes=gkey, phase=PFA)
                else:
                    S.op("dve", lambda e, src_=src_, dst_=dst_: e.tensor_copy(out=dst_, in_=src_),
                         reads=[("ps", b)], writes=gkey, phase=PFA)
                ev_ctr[0] += 1

        def chanfinal(ci):
            g, hh = combos_gh[ci]
            Gt = Gts[ci % NG]
            gkey = gkeys[ci % NG]

            def channel(k2l):
                b = newbank()
                fns = []
                for kc in range(KC):
                    fns.append(lambda e, b=b, kc=kc, k2l=k2l: e.matmul(
                        ps[b][:, kc * 256:(kc + 1) * 256], lhsT=Gt[:, k2l, kc * 128:(kc + 1) * 128],
                        rhs=csw[:, g, 128:384], start=True, stop=False))
                    fns.append(lambda e, b=b, kc=kc, k2l=k2l: e.matmul(
                        ps[b][:, kc * 256:(kc + 1) * 256], lhsT=Gt[:, 8 + k2l, kc * 128:(kc + 1) * 128],
                        rhs=csw[:, g, 0:256], start=False, stop=True))
                S.pe(fns, reads=gkey + [("csw", g)], writes=[("ps", b)], phase=PFA)
                r = h_ctr[0] % NR
                h_ctr[0] += 1
                k2 = 8 * hh + k2l
                dsto = Hq[:, r, :, :, :].rearrange("p k i d -> p (k i d)")
                srco = ps[b][:, 0:KC * 256]
                if h_ctr[0] % 2 == 0:
                    S.op("act", lambda e, dsto=dsto, srco=srco: e.activation(out=dsto, in_=srco, func=AF.Copy),
                         reads=[("ps", b)], writes=[("H", r)], phase=PFA)
                else:
                    S.op("dve", lambda e, dsto=dsto, srco=srco: e.tensor_copy(out=dsto, in_=srco),
                         reads=[("ps", b)], writes=[("H", r)], phase=PFA)
                return r

            def final(k2l, r, bF):
                fns = []
                n = 0
                for kc in range(KC):
                    for i in range(2):
                        fns.append(lambda e, kc=kc, i=i, r=r, k2l=k2l, bF=bF, n=n: e.matmul(
                            ps[bF][:, (k2l % 4) * 128:(k2l % 4 + 1) * 128], lhsT=Hq[:, r, kc, i, :],
                            rhs=Ekr[:, 8 * hh + k2l, kc, i, :], start=(n == 0), stop=(n == 2 * KC - 1)))
                        n += 1
                S.pe(fns, reads=[("H", r)] + ekkeys(r), writes=[("ps", bF)], phase=PFA)
                if k2l % 4 == 3:
                    k2base = 8 * hh + k2l - 3
                    dst_ = yaT(g).rearrange("p (n k) -> p k n", k=16)[:, k2base:k2base + 4, :]
                    src_ = ps[bF][:].rearrange("p (k n) -> p k n", k=4)
                    S.op("act", lambda e, dst_=dst_, src_=src_: e.activation(out=dst_, in_=src_, func=AF.Identity,
                                                                             bias=bfm_sb[:, g:g + 1]),
                         reads=[("ps", bF), "tabs"], writes=[("Y", g)], phase=[("AYU", PF)])

            return channel, final

        LAG = 3
        pend = []
        bFs = {}

        fin_ctr = [0]

        def do_final(item):
            fin, pk, pr, ci = item
            if pk % 4 == 0:
                bFs[ci] = 6 + fin_ctr[0] % 2
                fin_ctr[0] += 1
            fin(pk, pr, bFs[ci])

        bank_ring[:] = [0, 1, 2, 3, 4, 5]
        s2stage(0)
        s2stage(1)
        for ci in range(8):
            channel, final = chanfinal(ci)
            for k2l in range(8):
                r = channel(k2l)
                pend.append((final, k2l, r, ci))
                if len(pend) > LAG:
                    do_final(pend.pop(0))
            if ci + 2 < 8:
                s2stage(ci + 2)
        while pend:
            do_final(pend.pop(0))
        bank_ring[:] = list(range(8))

        if stage == 3:
            return
        sga = AR[:, 0:2048].rearrange("p (f n) -> p f n", f=4)
        sgb = AR[:, 2048:4096].rearrange("p (f n) -> p f n", f=4)
        tyb = AR[:, 4096:6144].rearrange("p (f n) -> p f n", f=4)
        vhat = AR[:, 6144:8192].rearrange("p (j c) -> p j c", j=4)
        sma = AY[:, 8192:12288].rearrange("p (k n) -> p k n", k=8)
        smb = AY[:, 12288:16384].rearrange("p (k n) -> p k n", k=8)
        phB = [("AR", PB)]
        phU = [("AYU", PB)]
        HK = [("hT0", kc) for kc in range(8)]

        pending = []

        def out_point(j):
            if pending:
                if j + 1 < 4:
                    pending[j + 1][0]()
                pending[j][1]()

        for t in range(4):
            for j in range(4):
                b = newbank()
                S.pe([lambda e, b=b, kc=kc, j=j: e.matmul(ps[b][:], lhsT=hT0[:, kc, j * 128:(j + 1) * 128],
                                                          rhs=w_in_sb[:, kc, V_OFF:V_OFF + 512],
                                                          start=(kc == 0), stop=(kc == 7)) for kc in range(8)],
                     reads=HK + ["w_in"], writes=[("ps", b)])
                S.op("dve", lambda e, b=b, j=j: e.bn_stats(out=bnst[:, j, :], in_=ps[b][:]),
                     reads=[("ps", b)], writes=[("bnst", j)])
                S.op("dve", lambda e, j=j: e.bn_aggr(out=bnag[:, j, 0:2], in_=bnst[:, j, :]),
                     reads=[("bnst", j)], writes=[("bnag", j)])
                S.op("act", lambda e, j=j: e.activation(out=bnag[:, j, 2:3], in_=bnag[:, j, 1:2], func=AF.Sqrt,
                                                         scale=1.0, bias=eps_sb[:, 0:1]),
                     reads=[("bnag", j), "eps"], writes=[("bnagS", j)])
                S.op("dve", lambda e, j=j: e.reciprocal(out=bnag[:, j, 2:3], in_=bnag[:, j, 2:3]),
                     reads=[("bnagS", j)], writes=[("bnag2", j)])
                S.op("dve", lambda e, j=j: e.scalar_tensor_tensor(out=bnag[:, j, 3:4], in0=bnag[:, j, 0:1], scalar=-1.0,
                                                                   in1=bnag[:, j, 2:3], op0=ALU.mult, op1=ALU.mult),
                     reads=[("bnag", j), ("bnag2", j)], writes=[("bnag3", j)])
                S.op("act", lambda e, b=b, j=j: e.activation(out=vhat[:, j, :], in_=ps[b][:], func=AF.Identity,
                                                             scale=bnag[:, j, 2:3], bias=bnag[:, j, 3:4]),
                     reads=[("ps", b), ("bnag2", j), ("bnag3", j)], writes=[("vhat", j)], phase=phB)

            def fm_group(off, fc):
                b = newbank()
                S.pe([lambda e, b=b, kc=kc: e.matmul(ps[b][:], lhsT=w_in_sb[:, kc, off + fc * 128: off + (fc + 1) * 128],
                                                     rhs=hT0[:, kc, :], start=(kc == 0), stop=(kc == 7)) for kc in range(8)],
                     reads=HK + ["w_in"], writes=[("ps", b)])
                return b

            out_point(0)
            for fc in range(4):
                b = fm_group(GB_OFF, fc)
                S.op("act", lambda e, b=b, fc=fc: e.activation(out=sgb[:, fc, :], in_=ps[b][:], func=AF.Silu),
                     reads=[("ps", b)], writes=[("sgb", fc)], phase=phB)
            out_point(1)
            for fc in range(4):
                b = fm_group(GA_OFF, fc)
                S.op("act", lambda e, b=b, fc=fc: e.activation(out=sga[:, fc, :], in_=ps[b][:], func=AF.Silu),
                     reads=[("ps", b)], writes=[("sga", fc)], phase=phB)
            out_point(2)
            for fc in range(4):
                b = fm_group(U_OFF, fc)
                S.op("dve", lambda e, b=b, fc=fc: e.tensor_tensor(out=sgb[:, fc, :], in0=ps[b][:], in1=sgb[:, fc, :], op=ALU.mult),
                     reads=[("ps", b), ("sgb", fc)], writes=[("sgb", fc)], phase=phB)
            if t + 1 < 4:
                fe1(rowsB(t + 1), FE_RING_B)
            elif u + 1 < 3 and stage == 0:
                fe1(rowsA_of(u + 1, 0), FE_RING_B)
            for fc in range(8):
                b = fm_group(MB_OFF, fc)
                S.op("act", lambda e, b=b, fc=fc: e.activation(out=smb[:, fc, :], in_=ps[b][:], func=AF.Sigmoid),
                     reads=[("ps", b)], writes=[("smb", fc)], phase=phU)
            out_point(3)
            for fc in range(8):
                b = fm_group(MA_OFF, fc)
                S.op("act", lambda e, b=b, fc=fc: e.activation(out=sma[:, fc, :], in_=ps[b][:], func=AF.Sigmoid),
                     reads=[("ps", b)], writes=[("sma", fc)], phase=phU)

            for g in range(4):
                S.op("pool", lambda e, g=g, t=t: e.tensor_tensor(out=sga[:, g, :], in0=yaT(g)[:, t * 512:(t + 1) * 512],
                                                                 in1=sga[:, g, :], op=ALU.mult),
                     reads=[("Y", g), ("sga", g)], writes=[("sga", g)], phase=phB + phU)

            if t + 1 < 4:
                fe2(hT0, "hT0", u)
            elif u + 1 < 3 and stage == 0:
                fe2(hT0, "hT0", u + 1)
                prefetched[u + 1] = True

            for h in range(4):
                b = newbank()
                S.pe([lambda e, b=b, h=h, j=j: e.matmul(ps[b][:, j * 128:(j + 1) * 128], lhsT=vhat[:, j, h * 128:(h + 1) * 128],
                                                        rhs=wsT[:, h, :], start=True, stop=True) for j in range(4)],
                     reads=[("vhat", j) for j in range(4)] + ["wsT"], writes=[("ps", b)], phase=phB)
                S.op("dve", lambda e, b=b, h=h: e.scalar_tensor_tensor(
                    out=tyb[:, h, :].rearrange("p (j q) -> p j q", j=4), in0=ps[b][:].rearrange("p (j q) -> p j q", j=4),
                    scalar=lng_sb[:, h:h + 1], in1=Ktab[:, h, :].unsqueeze(1).to_broadcast([128, 4, 128]),
                    op0=ALU.mult, op1=ALU.add),
                    reads=[("ps", b), "Ktab", "tabs"], writes=[("tyb", h)], phase=phB)
                S.op("pool", lambda e, h=h: e.tensor_tensor(out=tyb[:, h, :], in0=tyb[:, h, :], in1=sgb[:, h, :], op=ALU.mult),
                     reads=[("tyb", h), ("sgb", h)], writes=[("tyb", h)], phase=phB)

            for dmc in range(8):
                b = newbank()
                S.pe([lambda e, b=b, k=k, dmc=dmc: e.matmul(ps[b][:], lhsT=w_pa_sb[:, k, dmc * 128:(dmc + 1) * 128],
                                                            rhs=sga[:, k, :], start=(k == 0), stop=(k == 3)) for k in range(4)],
                     reads=[("sga", k) for k in range(4)] + ["w_pab"], writes=[("ps", b)], phase=phB)
                S.op("dve", lambda e, b=b, dmc=dmc: e.tensor_tensor(out=sma[:, dmc, :], in0=ps[b][:], in1=sma[:, dmc, :], op=ALU.mult),
                     reads=[("ps", b), ("sma", dmc)], writes=[("sma", dmc)], phase=phU)
            for dmc in range(8):
                b = newbank()
                S.pe([lambda e, b=b, k=k, dmc=dmc: e.matmul(ps[b][:], lhsT=w_pb_sb[:, k, dmc * 128:(dmc + 1) * 128],
                                                            rhs=tyb[:, k, :], start=(k == 0), stop=(k == 3)) for k in range(4)],
                     reads=[("tyb", k) for k in range(4)] + ["w_pab"], writes=[("ps", b)], phase=phB)
                S.op("dve", lambda e, b=b, dmc=dmc: e.tensor_tensor(out=smb[:, dmc, :], in0=ps[b][:], in1=smb[:, dmc, :], op=ALU.mult),
                     reads=[("ps", b), ("smb", dmc)], writes=[("smb", dmc)], phase=phU)
                S.op("pool", lambda e, dmc=dmc: e.tensor_tensor(out=sma[:, dmc, :], in0=sma[:, dmc, :], in1=smb[:, dmc, :], op=ALU.add),
                     reads=[("sma", dmc), ("smb", dmc)], writes=[("sma", dmc)], phase=phU)

            def make_chunk(t, j):
                st = {}
                row0 = u * UT + t * 512 + j * 128

                def load():
                    k = BE_RING[be_ctr[0] % len(BE_RING)]
                    be_ctr[0] += 1
                    st["k"] = k
                    S.dma("sp", [lambda e, k=k: e.dma_start(out=xy[:, k, :], in_=xu.ap()[row0:row0 + 128, :])],
                          "s_ld%d" % k, writes=[("xy", k)])

                def chunk(junk=junk):
                    k = st["k"]
                    si = st_ctr[0] % 8
                    st_ctr[0] += 1
                    for half in range(2):
                        b = newbank()
                        S.pe([lambda e, b=b, dmc=dmc, half=half: e.matmul(
                            ps[b][:], lhsT=sma[:, dmc, j * 128:(j + 1) * 128], rhs=w_out_sb[:, dmc, half * 512:(half + 1) * 512],
                            start=(dmc == 0), stop=(dmc == 7)) for dmc in range(8)],
                            reads=[("sma", dmc) for dmc in range(8)] + ["w_out"], writes=[("ps", b)], phase=phU)
                        S.op("dve", lambda e, b=b, k=k, half=half: e.tensor_tensor(
                            out=xy[:, k, half * 512:(half + 1) * 512], in0=ps[b][:], in1=xy[:, k, half * 512:(half + 1) * 512],
                            op=ALU.add),
                            reads=[("ps", b), ("xy", k)], writes=[("xy", k)])
                    S.op("act", lambda e, k=k, si=si: e.activation(out=junk, in_=xy[:, k, :], func=AF.Square,
                                                                   accum_out=stat[:, si, 0:1]),
                         reads=[("xy", k)], writes=[("st", si), "junk"], phase=phU)
                    S.op("act", lambda e, si=si: e.activation(out=stat[:, si, 1:2], in_=stat[:, si, 0:1], func=AF.Sqrt,
                                                               scale=1.0 / D, bias=eps_sb[:, 0:1]),
                         reads=[("st", si), "eps"], writes=[("st1", si)])
                    S.op("dve", lambda e, si=si: e.reciprocal(out=stat[:, si, 2:3], in_=stat[:, si, 1:2]),
                         reads=[("st1", si)], writes=[("st2", si)])
                    S.op("dve", lambda e, k=k, si=si: e.scalar_tensor_tensor(out=xy[:, k, :], in0=xy[:, k, :],
                                                                              scalar=stat[:, si, 2:3], in1=fg_bc[:],
                                                                              op0=ALU.mult, op1=ALU.mult),
                         reads=[("xy", k), ("st2", si), "tabs"], writes=[("xy", k)])
                    tok = S.dma("sp", [lambda e, k=k: e.dma_start(out=yu.ap()[row0:row0 + 128, :], in_=xy[:, k, :])],
                                "s_st%d" % k, reads=[("xy", k)], writes=[("yout", row0)])
                    out_tokens.append(tok)
                return load, chunk

            pending[:] = [make_chunk(t, j) for j in range(4)]
            pending[0][0]()

        if u + 1 < 3 and stage == 0:
            pending_x[:] = list(pending)
        else:
            for j in range(4):
                if j + 1 < 4:
                    pending[j + 1][0]()
                pending[j][1]()
        pending[:] = []

    for u in range(3):
        do_unit(u)
        if stage in (2, 3, 4, 21, 22, 23, 31, 32, 33, 34, 35):
            break

    last = {}
    for sk, v in out_tokens:
        last[sk] = max(last.get(sk, 0), v)
    for sk, v in last.items():
        S.streams["sp"].append(("wait", sk, v))

    with nc.Block() as block:
        def run(stream, e):
            for it in stream:
                if it[0] == "wait":
                    e.wait_ge(sems[it[1]], it[2])
                elif it[0] == "raw":
                    it[1](e)
                else:
                    ins = it[1](e)
                    ins.then_inc(sems[it[2]], it[3])

        @block.tensor
        def _(e):
            run(S.streams["pe"], e)

        @block.scalar
        def _(e):
            run(S.streams["act"], e)

        @block.vector
        def _(e):
            run(S.streams["dve"], e)

        @block.gpsimd
        def _(e):
            run(S.streams["pool"], e)

        @block.sync
        def _(e):
            run(S.streams["sp"], e)

    es.close()
    return nc


_NC_CACHE = {}
_STAGE = 0


def kernel(x_prompt, x_sample, c_prompt, c_sample, norm_g, w_ada, b_ada, w_in, w_fmix, b_fmix,
           sgu_ln_g, sgu_ln_b, w_s, b_s, w_pa, w_pb, w_out, final_g):
    f = lambda a: np.ascontiguousarray(np.asarray(a, dtype=np.float32))
    x_prompt, x_sample, c_prompt, c_sample = f(x_prompt), f(x_sample), f(c_prompt), f(c_sample)
    norm_g, w_ada, b_ada, w_in = f(norm_g)[0], f(w_ada)[0], f(b_ada)[0], f(w_in)[0]
    w_fmix, b_fmix = f(w_fmix)[0], f(b_fmix)[0]
    sgu_ln_g, sgu_ln_b, w_s, b_s = f(sgu_ln_g)[0], f(sgu_ln_b)[0], f(w_s)[0], f(b_s)[0]
    w_pa, w_pb, w_out, final_g = f(w_pa)[0], f(w_pb)[0], f(w_out)[0], f(final_g)

    if "nc" not in _NC_CACHE:
        _NC_CACHE["nc"] = build_nc(_STAGE)
    nc = _NC_CACHE["nc"]

    shared = {
        "w_ada": w_ada, "w_in": w_in, "w_pa": w_pa, "w_pb": w_pb, "w_out": w_out,
        "b_adaT": f(b_ada.reshape(24, 128).T), "norm_gT": f(norm_g.reshape(8, 128).T),
        "wfm": f(w_fmix.transpose(1, 0, 2).reshape(128, 512)),
        "b_fmixT": f(b_fmix.T), "ln_gT": f(sgu_ln_g.reshape(4, 128).T), "ln_b": sgu_ln_b,
        "w_s": f(w_s.reshape(512, 128)), "b_s": f(b_s.reshape(512)), "final_g": final_g,
    }
    in_maps = []
    for k in range(NCORES):
        q, hf = k // 2, k % 2
        cst = _consts(hf)
        c3 = np.stack([c_sample[q], c_prompt[2 * k], c_prompt[2 * k + 1]], axis=0)
        cTk = f(c3.reshape(3, 8, 128).transpose(2, 1, 0).reshape(128, 24))
        xuk = np.concatenate([x_sample[q, hf * UT:(hf + 1) * UT], x_prompt[2 * k], x_prompt[2 * k + 1]], axis=0)
        m = dict(shared)
        m.update({"xs_full": x_sample[q], "xu": f(xuk), "cT": cTk})
        m.update(cst)
        in_maps.append(m)

    res = run_bass_kernel_spmd(nc, in_maps, core_ids=list(range(NCORES)))
    y_prompt = np.empty_like(x_prompt)
    y_sample = np.empty_like(x_sample)
    for k in range(NCORES):
        q, hf = k // 2, k % 2
        yu = np.asarray(res.results[k]["yu"], dtype=np.float32)
        y_sample[q, hf * UT:(hf + 1) * UT] = yu[0:UT]
        y_prompt[2 * k] = yu[UT:2 * UT]
        y_prompt[2 * k + 1] = yu[2 * UT:3 * UT]
    return (y_prompt, y_sample)
```
